# Optimizing a Trainium2 kernel written in Bass

```python
import math
import jax
import jax.numpy as jnp
from jax import lax
import numpy as np

D_MODEL = 1024
BATCH = 1
SEQ = 16384
DEPTH = 1
DEC_BATCH = 32
DEC_SEQ = 1
PAST_LEN = 16384
PAGE_SIZE = 128

W_ATT = D_MODEL // 2
HD_ATT = 64
H_ATT = W_ATT // HD_ATT
DILATIONS = ((128, 1), (512, 4), (2048, 16))
MAX_WINDOW = max(w for w, _ in DILATIONS)
Q_BLOCK = 128
N_BUCKETS = 32
MAX_EXACT = N_BUCKETS // 2
BUCKET_MAX_DIST = MAX_WINDOW
W_RET = D_MODEL - W_ATT
H_RET = 4
DK_RET = W_RET // H_RET
DV_RET = W_RET // H_RET
RET_CHUNK = 128
ROPE_BASE = 10000.0
W_MIX = W_ATT + W_RET
D_FF = 4 * D_MODEL
SPLITS = (W_ATT, W_ATT, W_ATT, H_RET * DK_RET, H_RET * DK_RET, H_RET * DV_RET, H_RET * DV_RET)
D_IN = sum(SPLITS)
ALPHA = (2.0 * DEPTH) ** 0.25
BETA = (8.0 * DEPTH) ** -0.25
LN_EPS = 1e-5
GN_EPS = 1e-6

kernel_name = 'hybrid_dilated_attn_retention_decoder_step'

F32 = jnp.float32


def layernorm(x, g, b):
    xf = x.astype(F32)
    mu = xf.mean(-1, keepdims=True)
    var = jnp.square(xf - mu).mean(-1, keepdims=True)
    return ((xf - mu) * lax.rsqrt(var + LN_EPS) * g.astype(F32) + b.astype(F32)).astype(x.dtype)


def t5_bucket(dist):
    is_small = dist < MAX_EXACT
    d_f = jnp.maximum(dist, 1).astype(F32)
    large = MAX_EXACT + (jnp.log(d_f / MAX_EXACT) / math.log(BUCKET_MAX_DIST / MAX_EXACT)
                         * (N_BUCKETS - MAX_EXACT)).astype(jnp.int32)
    large = jnp.minimum(large, N_BUCKETS - 1)
    return jnp.where(is_small, dist, large)


def softmax_stats(logits):
    m = logits.max(-1, keepdims=True)
    p = jnp.exp(logits - m)
    s = p.sum(-1, keepdims=True)
    return p / s, (m + jnp.log(s))[..., 0]


def in_projection(x, w_in):
    B, T, _ = x.shape
    u = jnp.einsum('btd,de->bte', x, w_in)
    offs = np.cumsum(SPLITS)[:-1].tolist()
    q_a, k_a, v_a, q_r, k_r, v_r, g_r = jnp.split(u, offs, axis=-1)
    return (q_a.reshape(B, T, H_ATT, HD_ATT), k_a.reshape(B, T, H_ATT, HD_ATT),
            v_a.reshape(B, T, H_ATT, HD_ATT), q_r.reshape(B, T, H_RET, DK_RET),
            k_r.reshape(B, T, H_RET, DK_RET), v_r.reshape(B, T, H_RET, DV_RET), g_r)


def dilated_branch_prompt(q, k, v, window, dil, rel_bias):
    B, S, H, Dh = q.shape
    sub_win = window // dil
    L = S // dil
    Lp = -(-L // Q_BLOCK) * Q_BLOCK
    nb = Lp // Q_BLOCK
    N = B * dil

    def to_sub(t):
        t = t.astype(F32).reshape(B, L, dil, H, Dh).transpose(0, 2, 1, 3, 4).reshape(N, L, H, Dh)
        return jnp.pad(t, ((0, 0), (0, Lp - L), (0, 0), (0, 0)))

    def band(t):
        prev = jnp.pad(t, ((0, 0), (Q_BLOCK, 0), (0, 0), (0, 0)))[:, :Lp]
        return jnp.concatenate([prev.reshape(N, nb, Q_BLOCK, H, Dh),
                                t.reshape(N, nb, Q_BLOCK, H, Dh)], axis=2)

    qs, ks, vs = to_sub(q), to_sub(k), to_sub(v)
    qb = qs.reshape(N, nb, Q_BLOCK, H, Dh)
    kb, vb = band(ks), band(vs)
    rel = jnp.arange(Q_BLOCK)[:, None] + Q_BLOCK - jnp.arange(2 * Q_BLOCK)[None, :]
    key_idx = (jnp.arange(nb)[:, None, None] * Q_BLOCK
               + jnp.arange(2 * Q_BLOCK)[None, None, :] - Q_BLOCK)
    valid = (rel >= 0) & (rel <= sub_win) & (key_idx >= 0)
    bias = rel_bias[t5_bucket(jnp.clip(rel, 0, sub_win) * dil)].astype(F32).transpose(2, 0, 1)
    logits = jnp.einsum('nbqhd,nbkhd->nbhqk', qb, kb) * HD_ATT ** -0.5 + bias[None, None]
    logits = jnp.where(valid[None, :, None], logits, -jnp.inf)
    p, lse = softmax_stats(logits)
    o = jnp.einsum('nbhqk,nbkhd->nbqhd', p, vb)
    o = o.reshape(N, Lp, H, Dh)[:, :L].reshape(B, dil, L, H, Dh).transpose(0, 2, 1, 3, 4)
    lse = lse.transpose(0, 1, 3, 2).reshape(N, Lp, H)[:, :L].reshape(B, dil, L, H).transpose(0, 2, 1, 3)
    return o.reshape(B, S, H, Dh), lse.reshape(B, S, H)


def dilated_branch_sample(q, k_all, v_all, window, dil, rel_bias):
    B, T, H, Dh = q.shape
    W = k_all.shape[1] - T
    taps = jnp.arange(window // dil + 1)
    idx = W + jnp.arange(T)[:, None] - taps[None, :] * dil
    valid = idx >= 0
    idx = jnp.maximum(idx, 0)
    kg = k_all.astype(F32)[:, idx]
    vg = v_all.astype(F32)[:, idx]
    bias = rel_bias[t5_bucket(taps * dil)].astype(F32).T
    logits = jnp.einsum('bthd,btjhd->bthj', q.astype(F32), kg) * HD_ATT ** -0.5 + bias[None, None]
    logits = jnp.where(valid[None, :, None, :], logits, -jnp.inf)
    p, lse = softmax_stats(logits)
    o = jnp.einsum('bthj,btjhd->bthd', p, vg)
    return o, lse


def mix_dilations(branches):
    lse = jnp.stack([b[1] for b in branches], axis=-1)
    w = jax.nn.softmax(lse, axis=-1)
    return sum(w[..., i, None] * o for i, (o, _) in enumerate(branches))


def rotary(x, pos):
    half = x.shape[-1] // 2
    inv_freq = 1.0 / (ROPE_BASE ** jnp.linspace(0.0, 1.0, half, dtype=F32))
    ang = pos.astype(F32)[:, None] * inv_freq[None, :]
    cos, sin = jnp.cos(ang)[None, :, None, :], jnp.sin(ang)[None, :, None, :]
    x1, x2 = x[..., :half].astype(F32), x[..., half:].astype(F32)
    return jnp.concatenate([x1 * cos - x2 * sin, x1 * sin + x2 * cos], axis=-1)


def log_gamma():
    return jnp.log(1.0 - jnp.exp2(-5.0 - jnp.arange(H_RET, dtype=F32)))


def retention_chunk(q, k, v, state):
    C = q.shape[1]
    lg = log_gamma()
    n = jnp.arange(C, dtype=F32)
    diff = n[:, None] - n[None, :]
    decay = jnp.where(diff >= 0, jnp.exp(lg[:, None, None] * jnp.maximum(diff, 0.0)), 0.0)
    scores = jnp.einsum('bqhd,bkhd->bhqk', q, k) * decay
    o_inner = jnp.einsum('bhqk,bkhv->bqhv', scores, v)
    o_cross = jnp.einsum('bqhd,bhdv->bqhv', q, state) * jnp.exp(lg[None, :] * (n[:, None] + 1.0))[None, :, :, None]
    k_dec = k * jnp.exp(lg[None, :] * (C - 1.0 - n[:, None]))[None, :, :, None]
    new_state = jnp.exp(lg * C)[None, :, None, None] * state + jnp.einsum('bkhd,bkhv->bhdv', k_dec, v)
    return o_inner + o_cross, new_state


def retention_prompt(q, k, v):
    B, S, H, Dk = q.shape
    nc = S // RET_CHUNK

    def chunks(t):
        return t.reshape(B, nc, RET_CHUNK, H, t.shape[-1]).transpose(1, 0, 2, 3, 4)

    def step(state, xs):
        qc, kc, vc = xs
        o, state = retention_chunk(qc, kc, vc, state)
        return state, o

    s0 = jnp.zeros((B, H, DK_RET, DV_RET), F32)
    s_fin, o = lax.scan(step, s0, (chunks(q), chunks(k), chunks(v)))
    return o.transpose(1, 0, 2, 3, 4).reshape(B, S, H, DV_RET), s_fin


def retention_output(o, g):
    B, T = o.shape[:2]
    mu = o.mean(-1, keepdims=True)
    var = jnp.square(o - mu).mean(-1, keepdims=True)
    y = ((o - mu) * lax.rsqrt(var + GN_EPS)).reshape(B, T, W_RET)
    return jax.nn.silu(g.astype(F32)) * y


def layer_output(x, o_att, y_ret, w_out, ln1_g, ln1_b, w_up, w_down, ln2_g, ln2_b):
    B, T, _ = x.shape
    heads = jnp.concatenate([o_att.reshape(B, T, W_ATT), y_ret], axis=-1).astype(x.dtype)
    mix = jnp.einsum('bte,ed->btd', heads, w_out)
    x1 = layernorm(ALPHA * x + mix, ln1_g, ln1_b)
    h = jnp.square(jax.nn.relu(jnp.einsum('btd,df->btf', x1, w_up)))
    ffn = jnp.einsum('btf,fd->btd', h, w_down)
    return layernorm(ALPHA * x1 + ffn, ln2_g, ln2_b)


def setup_inputs(seed: int = 0) -> dict:
    key = jax.random.key(seed)
    ks = jax.random.split(key, 16)
    nrm = jax.random.normal
    win_buf = min(MAX_WINDOW, PAST_LEN)
    col_scale = np.ones((D_IN,), np.float32)
    offs = np.cumsum((0,) + SPLITS)
    col_scale[offs[2]:offs[3]] = BETA
    col_scale[offs[5]:offs[6]] = BETA
    return {
        'x_prompt': nrm(ks[0], (BATCH, SEQ, D_MODEL), F32),
        'x_sample': nrm(ks[1], (DEC_BATCH, DEC_SEQ, D_MODEL), F32),
        'cache_kv_win': nrm(ks[2], (DEPTH, DEC_BATCH, win_buf, 2, H_ATT, HD_ATT), F32),
        'state_ret': 0.5 * nrm(ks[3], (DEPTH, DEC_BATCH, H_RET, DK_RET, DV_RET), F32),
        'w_in': nrm(ks[4], (DEPTH, D_MODEL, D_IN), F32) * D_MODEL ** -0.5 * jnp.asarray(col_scale),
        'rel_bias': 0.5 * nrm(ks[5], (N_BUCKETS, H_ATT), F32),
        'w_out': nrm(ks[6], (DEPTH, W_MIX, D_MODEL), F32) * W_MIX ** -0.5 * BETA,
        'ln1_g': 1.0 + 0.05 * nrm(ks[7], (DEPTH, D_MODEL), F32),
        'ln1_b': 0.02 * nrm(ks[8], (DEPTH, D_MODEL), F32),
        'w_up': nrm(ks[9], (DEPTH, D_MODEL, D_FF), F32) * D_MODEL ** -0.5,
        'w_down': nrm(ks[10], (DEPTH, D_FF, D_MODEL), F32) * D_FF ** -0.5 * BETA,
        'ln2_g': 1.0 + 0.05 * nrm(ks[11], (DEPTH, D_MODEL), F32),
        'ln2_b': 0.02 * nrm(ks[12], (DEPTH, D_MODEL), F32),
    }


def reference(x_prompt, x_sample, cache_kv_win, state_ret, w_in, rel_bias, w_out,
              ln1_g, ln1_b, w_up, w_down, ln2_g, ln2_b):
    pos_p = jnp.arange(x_prompt.shape[1], dtype=jnp.int32)
    pos_s = PAST_LEN + jnp.arange(x_sample.shape[1], dtype=jnp.int32)
    hp, hs = x_prompt, x_sample
    kv_p_list, kv_s_list, st_p_list, st_s_list = [], [], [], []
    for l in range(DEPTH):
        q_a, k_a, v_a, q_r, k_r, v_r, g_r = in_projection(hp, w_in[l])
        o_att = mix_dilations([dilated_branch_prompt(q_a, k_a, v_a, w, d, rel_bias) for w, d in DILATIONS])
        o_ret, st_p = retention_prompt(rotary(q_r, pos_p), rotary(k_r, pos_p) * DK_RET ** -0.5,
                                       v_r.astype(F32))
        y_ret = retention_output(o_ret, g_r)
        win_p = min(MAX_WINDOW, hp.shape[1])
        kv_p_list.append(jnp.stack([k_a, v_a], axis=2)[:, -win_p:])
        st_p_list.append(st_p)
        hp_next = layer_output(hp, o_att, y_ret, w_out[l], ln1_g[l], ln1_b[l], w_up[l], w_down[l],
                               ln2_g[l], ln2_b[l])

        q_a, k_a, v_a, q_r, k_r, v_r, g_r = in_projection(hs, w_in[l])
        kv_buf = cache_kv_win[l]
        k_all = jnp.concatenate([kv_buf[:, :, 0].astype(k_a.dtype), k_a], axis=1)
        v_all = jnp.concatenate([kv_buf[:, :, 1].astype(v_a.dtype), v_a], axis=1)
        o_att = mix_dilations([dilated_branch_sample(q_a, k_all, v_all, w, d, rel_bias) for w, d in DILATIONS])
        o_ret, st_s = retention_chunk(rotary(q_r, pos_s), rotary(k_r, pos_s) * DK_RET ** -0.5,
                                      v_r.astype(F32), state_ret[l].astype(F32))
        y_ret = retention_output(o_ret, g_r)
        kv_s_list.append(jnp.stack([k_a, v_a], axis=2))
        st_s_list.append(st_s)
        hs_next = layer_output(hs, o_att, y_ret, w_out[l], ln1_g[l], ln1_b[l], w_up[l], w_down[l],
                               ln2_g[l], ln2_b[l])
        hp, hs = hp_next, hs_next
    kv_win_prompt = jnp.stack(kv_p_list)
    kv_win_sample = jnp.stack(kv_s_list)
    state_ret_prompt = jnp.stack(st_p_list)
    state_ret_sample = jnp.stack(st_s_list)
    return (hp, hs, kv_win_prompt, kv_win_sample, state_ret_prompt, state_ret_sample)
```

```python
import math
import numpy as np
import concourse.bass as bass
import concourse.mybir as mybir
from concourse.bass_utils import run_bass_kernel_spmd

F32 = mybir.dt.float32
BF16 = mybir.dt.bfloat16
ALU = mybir.AluOpType
AF = mybir.ActivationFunctionType
AX = mybir.AxisListType

SAME_ENGINE_SYNC = True
NCORES = 8
NHALO = 42
HJ = (6, 11, 21, 42)
T = 2048
ALPHA = 2.0 ** 0.25
ROW = 1544
DILS = (1, 4, 16)
GAM = [1.0 - 2.0 ** (-5.0 - h) for h in range(4)]


class Op:
    __slots__ = ("eng", "fn", "deps", "is_dma", "key", "signal", "cnt", "out", "desc")


class Prog:
    def __init__(self, nc, fence_tile):
        self.nc = nc
        self.ops = []
        self.last_w = {}
        self.readers = {}
        self.eng = {"pe": nc.tensor, "act": nc.scalar, "dve": nc.vector,
                    "pool": nc.gpsimd, "sp": nc.sync}
        self.out_dmas = []
        self.last_op = {}
        self.dmas_since = []
        self.fence_op = None
        self.synced = set()
        self.fence_tile = fence_tile
        self.dma_chain = {"pool": 6}
        self.dma_hist = {}

    def _mk(self, eng, fn, dma, out):
        op = Op()
        op.eng = eng
        op.fn = fn
        op.is_dma = dma is not None
        op.key = dma
        op.signal = False
        op.cnt = 0
        op.out = out
        return op

    def add(self, eng, fn, r=(), w=(), dma=None, out=False):
        op = self._mk(eng, fn, dma, out)
        op.desc = "%s r=%s w=%s" % (eng, list(r), list(w))
        psr = [x for x in r if x.startswith("ps")]
        if psr:
            r = [x for x in r if not x.startswith("ps")]
            w = list(w) + psr
        deps = []
        seen = set()

        def push(o):
            if o is not None and id(o) not in seen:
                seen.add(id(o))
                deps.append(o)
        for x in r:
            push(self.last_w.get(x))
        for x in w:
            push(self.last_w.get(x))
            for rd in self.readers.get(x, ()):
                push(rd)
        if op.is_dma and self.dma_chain.get(eng, 0) > 0:
            lst = self.dma_hist.setdefault(eng, [])
            k = self.dma_chain[eng]
            if len(lst) >= k:
                push(lst[-k])
            lst.append(op)
        if self.fence_op is not None and eng not in self.synced:
            push(self.fence_op)
            self.synced.add(eng)
        op.deps = deps
        for x in r:
            self.readers.setdefault(x, []).append(op)
        for x in w:
            self.last_w[x] = op
            self.readers[x] = []
        self.ops.append(op)
        self.last_op[eng] = op
        if op.is_dma:
            self.dmas_since.append(op)
        if out:
            self.out_dmas.append(op)
        return op

    def fence(self):
        ft = self.fence_tile
        op = self._mk("pool", lambda e: e.memset(ft, 0.0), None, False)
        op.desc = "FENCE"
        deps = [o for o in self.last_op.values()] + list(self.dmas_since)
        op.deps = deps
        self.ops.append(op)
        self.last_op["pool"] = op
        self.fence_op = op
        self.synced = {"pool"}
        self.dmas_since = []

    def _need(self, a, b):
        if a.is_dma or b.is_dma:
            return True
        if a.eng != b.eng:
            return True
        if a.eng == "pe":
            return False
        return SAME_ENGINE_SYNC

    def emit(self):
        nc = self.nc
        for b in self.ops:
            b.deps = [a for a in b.deps if self._need(a, b)]
            for a in b.deps:
                a.signal = True
        for o in self.ops:
            if o.is_dma:
                o.signal = True
        eng_sem, dma_sem, eng_cnt, dma_cnt = {}, {}, {}, {}
        for o in self.ops:
            if not o.signal:
                continue
            if o.is_dma:
                if o.key not in dma_sem:
                    dma_sem[o.key] = nc.alloc_semaphore(name="d%d" % len(dma_sem))
                    dma_cnt[o.key] = 0
                dma_cnt[o.key] += 16
                o.cnt = dma_cnt[o.key]
            else:
                if o.eng not in eng_sem:
                    eng_sem[o.eng] = nc.alloc_semaphore(name="e_" + o.eng)
                    eng_cnt[o.eng] = 0
                eng_cnt[o.eng] += 1
                o.cnt = eng_cnt[o.eng]
        self.n_sems = len(eng_sem) + len(dma_sem)
        waited = {e: {} for e in self.eng}
        acts = {e: [] for e in self.eng}
        n_wait = 0
        for b in self.ops:
            wl = waited[b.eng]
            need = {}
            for a in b.deps:
                sem = dma_sem[a.key] if a.is_dma else eng_sem[a.eng]
                k = id(sem)
                if a.cnt > wl.get(k, 0) and a.cnt > need.get(k, (None, 0))[1]:
                    need[k] = (sem, a.cnt)
            for k, (sem, val) in need.items():
                acts[b.eng].append(("w", sem, val))
                wl[k] = val
                n_wait += 1
            if b.signal:
                acts[b.eng].append(("i", b.fn, dma_sem[b.key] if b.is_dma else eng_sem[b.eng],
                                    16 if b.is_dma else 1, b.desc, b.cnt))
            else:
                acts[b.eng].append(("i", b.fn, None, 0, b.desc, 0))
        fin = {}
        for o in self.ops:
            if not o.is_dma:
                continue
            sem = dma_sem[o.key]
            if o.cnt > fin.get(id(sem), (None, 0))[1]:
                fin[id(sem)] = (sem, o.cnt)
        for k, (sem, val) in fin.items():
            acts["sp"].append(("w", sem, val))
        self.n_wait = n_wait
        self.n_ops = len(self.ops)
        self.acts = acts
        self.sem_names = {id(v): k for k, v in list(eng_sem.items()) + list(dma_sem.items())}

        def run(e, lst):
            for a in lst:
                if a[0] == "w":
                    e.wait_ge(a[1], a[2])
                else:
                    ins = a[1](e)
                    if a[2] is not None:
                        ins.then_inc(a[2], a[3])
        with nc.Block() as block:
            @block.sync
            def _(e):
                run(e, acts["sp"])

            @block.tensor
            def _(e):
                run(e, acts["pe"])

            @block.scalar
            def _(e):
                run(e, acts["act"])

            @block.vector
            def _(e):
                run(e, acts["dve"])

            @block.gpsimd
            def _(e):
                run(e, acts["pool"])


def f_dma(out, in_):
    return lambda e: e.dma_start(out=out, in_=in_)


def f_mm(out, lhsT, rhs, start, stop):
    return lambda e: e.matmul(out, lhsT=lhsT, rhs=rhs, start=start, stop=stop)


def f_tr(out, in_, ident):
    return lambda e: e.transpose(out, in_, ident)


def f_copy(out, in_):
    def fn(e):
        if hasattr(e, "tensor_copy"):
            return e.tensor_copy(out=out, in_=in_)
        return e.activation(out=out, in_=in_, func=AF.Copy)
    return fn


def f_act(out, in_, func, bias=None, scale=None, accum_out=None):
    kw = {}
    if bias is not None:
        kw["bias"] = bias
    if scale is not None:
        kw["scale"] = scale
    if accum_out is not None:
        kw["accum_out"] = accum_out
    return lambda e: e.activation(out=out, in_=in_, func=func, **kw)


def f_tt(out, in0, in1, op):
    return lambda e: e.tensor_tensor(out=out, in0=in0, in1=in1, op=op)


def f_ts(out, in0, s1, s2, op0, op1=None):
    if op1 is None:
        return lambda e: e.tensor_scalar(out=out, in0=in0, scalar1=s1, scalar2=None, op0=op0)
    return lambda e: e.tensor_scalar(out=out, in0=in0, scalar1=s1, scalar2=s2, op0=op0, op1=op1)


def f_stt(out, in0, scalar, in1, op0, op1):
    return lambda e: e.scalar_tensor_tensor(out=out, in0=in0, scalar=scalar, in1=in1, op0=op0, op1=op1)


def f_red(out, in_, op, axis=AX.X):
    return lambda e: e.tensor_reduce(out=out, in_=in_, axis=axis, op=op)


def f_memset(ap, v):
    return lambda e: e.memset(ap, v)


def f_recip(out, in_):
    return lambda e: e.reciprocal(out=out, in_=in_)


def f_bnstats(out, in_):
    return lambda e: e.bn_stats(out=out, in_=in_)


def f_bnaggr(out, in_):
    return lambda e: e.bn_aggr(out=out, in_=in_)


def build(enable):
    nc = bass.Bass("TRN2", target_bir_lowering=False)

    def din(name, shape):
        return nc.dram_tensor(name, list(shape), F32, kind="ExternalInput")

    def dout(name, shape):
        return nc.dram_tensor(name, list(shape), F32, kind="ExternalOutput")

    def dscr(name, shape, dt):
        return nc.dram_tensor(name, list(shape), dt, kind="Internal")

    xo = din("xo", [T, 1024]).ap()
    xh = din("xh", [NHALO * 128, 1024]).ap()
    xs = din("xs", [128, 1024]).ap()
    cache = din("cache", [4, 2048, 1024])
    state = din("state", [4, 4, 128, 128]).ap()
    w_in = din("w_in", [1024, 3584]).ap()
    relb = din("relb", [32, 32]).ap()
    w_out = din("w_out", [1024, 1024]).ap()
    ln1g = din("ln1g", [1, 1024])
    ln1b = din("ln1b", [1, 1024])
    w_up = din("w_up", [1024, 4096]).ap()
    w_down = din("w_down", [4096, 1024]).ap()
    ln2g = din("ln2g", [1, 1024])
    ln2b = din("ln2b", [1, 1024])
    c_ident = din("c_ident", [128, 128]).ap()
    c_jflip = din("c_jflip", [128, 128]).ap()
    c_cs = din("c_cs", [(NHALO + 17) * 128, 128]).ap()
    c_vtab = din("c_vtab", [128, 33]).ap()
    c_dmt = din("c_dmt", [128, 512]).ap()
    c_qdec = din("c_qdec", [128, 512]).ap()
    c_kdec = din("c_kdec", [128, 512]).ap()
    c_oh = din("c_oh", [32, 1152 + 387]).ap()
    c_fmask = din("c_fmask", [8, 1152]).ap()
    c_gam = din("c_gam", [128, 512]).ap()
    c_delta = din("c_delta", [128, 4]).ap()
    c_drow = din("c_drow", [128, 16]).ap()

    y_p = dout("y_p", [T, 1024]).ap()
    y_s = dout("y_s", [4, 1024]).ap()
    kvp = dout("kvp", [T, 1024]).ap()
    kvs = dout("kvs", [4, 1024]).ap()
    srp = dout("srp", [4, 128, 128]).ap()
    srs = dout("srs", [4, 4, 128, 128]).ap()

    QKV = dscr("QKV", [4096 + 128, ROW], BF16)
    OL = dscr("OL", [T, 512], F32).ap()
    GS = dscr("GS", [T + 128, 512], BF16).ap()
    ATT = dscr("ATT", [64, 8 * T], BF16).ap()
    FSC = dscr("FSC", [8, 1152], F32)
    SQ = dscr("SQ", [3, 4, 512], F32).ap()
    SATT = dscr("SATT", [32, 64], F32).ap()

    def qkv_rows(base, step, c0, c1):
        return bass.AP(QKV, base * ROW + c0, [[step * ROW, 128], [1, c1 - c0]])

    def sb(name, cols, dt, parts=128):
        return nc.alloc_sbuf_tensor(name, [parts, cols], dt).ap()

    ps = [nc.alloc_psum_tensor("ps%d" % i, [128, 512], F32).ap() for i in range(8)]
    psb = [p.bitcast(BF16) for p in ps]

    fence_tile = sb("fence_t", 8, F32)
    P = Prog(nc, fence_tile)

    identb = sb("identb", 128, BF16)
    identf = sb("identf", 128, F32)
    jflip = sb("jflipf", 128, F32)
    onesf = sb("onesf", 128, F32)
    e64 = sb("e64", 64, F32)
    QdecT = sb("QdecT", 4 * 2176, BF16).rearrange("p (h t) -> p h t", h=4)
    w_out_sb = sb("w_out_sb", 8 * 1024, BF16).rearrange("p (c n) -> p c n", c=8)
    lng = sb("lng", 1024, F32)
    lnb = sb("lnb", 1024, F32)
    S_f = sb("S_f", 512, F32)
    S_b = sb("S_b", 512, BF16)
    negm = sb("negm", 1, F32)
    vtab = sb("vtab", 33, F32)
    gam = sb("gam", 512, F32)
    delta = sb("delta", 4, F32)
    drow = sb("drow", 16, F32)
    stats = sb("stats", 64, F32)
    att_s = sb("att_s", 512, BF16).rearrange("p (c t) -> p c t", c=4)
    oret_s = sb("oret_s", 512, F32)
    X = sb("arenaX", 32768, BF16)
    Y = sb("arenaY", 34816, BF16)

    class Arena:
        def __init__(self, ap, size):
            self.ap = ap
            self.size = size
            self.off = 0

        def reset(self):
            self.off = 0

        def get(self, cols, dt):
            nb = cols * (2 if dt == BF16 else 4)
            nb = (nb + 31) // 32 * 32
            assert self.off + nb <= self.size, (self.off, nb, self.size)
            a = self.ap[:, self.off // 2:(self.off + nb) // 2]
            self.off += nb
            if dt == F32:
                return a.bitcast(F32)[:, 0:cols]
            return a[:, 0:cols]

    AX_ = Arena(X, 57344)
    AY_ = Arena(Y, 69632)

    P.add("pool", f_dma(identb, c_ident), w=["identb"], dma="c0")
    P.add("sp", f_dma(identf, c_ident), w=["identf"], dma="k1_1")
    P.add("sp", f_dma(jflip, c_jflip), w=["jflip"], dma="k1_2")
    P.add("sp", f_dma(vtab, c_vtab), w=["vtab"], dma="k1_4")
    P.add("sp", f_dma(gam, c_gam), w=["gam"], dma="k1_5")
    P.add("sp", f_dma(delta, c_delta), w=["delta"], dma="k1_6")
    P.add("sp", f_dma(drow, c_drow), w=["drow"], dma="k1_7")
    P.add("dve", f_memset(onesf, 1.0), w=["onesf"])
    P.add("dve", f_memset(e64, 0.0), w=["e64"])
    P.add("dve", f_memset(e64[64:65, :], 1.0), r=["e64"], w=["e64"])
    P.add("dve", f_memset(negm, 0.0), w=["negm"])
    P.add("dve", f_memset(S_f, 0.0), w=["S_f"])
    P.add("dve", f_memset(S_b, 0.0), w=["S_b"])
    P.add("dve", f_memset(oret_s, 0.0), w=["oret_s"])
    P.add("dve", f_memset(att_s, 0.0), w=["att_s"])

    AX_.reset()
    AY_.reset()
    w_in_sb = AX_.get(8 * 3584, BF16).rearrange("p (c n) -> p c n", c=8)
    for c in range(8):
        P.add("pool", f_dma(w_in_sb[:, c, :], w_in[c * 128:(c + 1) * 128, :]), w=["w_in%d" % c], dma="w_in%d" % c)
    cs_t = [AY_.get(128, F32) for _ in range(3)]
    dmt = AY_.get(512, F32)
    qdec = AY_.get(512, F32)
    kdec = AY_.get(512, F32)
    P.add("sp", f_dma(dmt, c_dmt), w=["dmt"], dma="k1_9")
    P.add("sp", f_dma(qdec, c_qdec), w=["qdec"], dma="k1_10")
    P.add("sp", f_dma(kdec, c_kdec), w=["kdec"], dma="k1_11")
    xb = [AY_.get(1024, BF16) for _ in range(2)]
    xT = [AY_.get(1024, BF16) for _ in range(2)]
    qkv_st = [AY_.get(ROW, BF16) for _ in range(2)]
    kv_f = [AY_.get(1024, F32) for _ in range(2)]
    rt = [AY_.get(256, F32) for _ in range(4)]
    qrot_bb = [AY_.get(512, BF16) for _ in range(3)]
    krot_bb = [AY_.get(512, BF16) for _ in range(3)]
    kdec_b = AY_.get(512, BF16)
    krT = AY_.get(512, BF16)
    qrT = AY_.get(512, BF16)
    v_bb = [AY_.get(512, BF16) for _ in range(3)]
    g_st = [AY_.get(512, BF16) for _ in range(2)]
    sT_b = AY_.get(512, BF16)
    sqt = AY_.get(512, F32)
    redq = AY_.get(8, F32)
    mxq = AY_.get(8, F32)
    mxk = AY_.get(8, F32)
    mfin = AY_.get(8, F32)
    P.add("pool", f_memset(mxq, 0.0), w=["mxq"])
    P.add("pool", f_memset(mxk, 0.0), w=["mxk"])
    o_st = [AY_.get(512, F32) for _ in range(2)]
    xtail = X[:, 28672:32768].bitcast(F32)
    qrot_f = xtail[:, 0:512]
    krot_f = xtail[:, 512:1024]
    v_f = xtail[:, 1024:1536]
    sq_f = xtail[:, 1536:2048]

    proj_banks = [1, 2, 3, 4]
    st_ = {"pc": 0, "tc": 0, "rc": 0}

    def nbank(kind):
        if kind == "proj":
            b = proj_banks[st_["pc"] % 4]
            st_["pc"] += 1
        else:
            b = (6, 7)[st_["rc"] % 2]
            st_["rc"] += 1
        return b

    def prefetchA(kind, idx, n):
        s = n % 2
        src = {"halo": xh, "own": xo, "samp": xs}[kind]
        rows = src[idx * 128:(idx + 1) * 128, :] if kind != "samp" else src
        P.add("pool", f_dma(xb[s], rows), w=["xb%d" % s], dma="xb%d" % s)
        csrow = {"halo": idx, "own": NHALO + idx, "samp": NHALO + 16}[kind]
        s3c = n % 3
        P.add("sp", f_dma(cs_t[s3c], c_cs[csrow * 128:(csrow + 1) * 128, :]), w=["cs%d" % s3c], dma="cs%d" % s3c)

    def phaseA_tile(kind, idx, n):
        s = n % 2
        s3 = n % 3
        qrot_b = qrot_bb[s3]
        krot_b = krot_bb[s3]
        v_b = v_bb[s3]
        sfx = "" if kind == "samp" else str(s3)
        for c in range(8):
            P.add("pe", f_tr(psb[0][:, c * 128:(c + 1) * 128], xb[s][:, c * 128:(c + 1) * 128], identb),
                  r=["xb%d" % s, "identb"], w=["ps0"])
        P.add("act", f_copy(xT[s], psb[0]), r=["ps0"], w=["xTa%d" % s, "xTb%d" % s])
        def front_b():
            jb = NHALO - idx if kind == "halo" else 0
            if kind == "halo":
                chunks = ([1, 2] if jb <= 16 else []) + [4, 5]
                h0 = [h for h in range(4) if jb <= HJ[h]][0]
            else:
                chunks = [0, 1, 2, 3, 4, 5, 6]
                h0 = 0
            nh = 4 - h0
            if "chunks" in enable:
                chunks = enable["chunks"]
            tcol = {"halo": idx - (NHALO - 16), "own": 16 + idx, "samp": 32}[kind]
            row0 = {"halo": (idx - (NHALO - 16)) * 128, "own": 2048 + idx * 128, "samp": 4096}[kind]
            for j in chunks:
                b = nbank("proj")
                pb = "ps%d" % b
                c0_ = h0 * 128 if j in (4, 5) else 0
                for c in range(8):
                    P.add("pe", f_mm(ps[b][:, c0_:512], xT[s][:, c * 128:(c + 1) * 128],
                                     w_in_sb[:, c, j * 512 + c0_:(j + 1) * 512], c == 0, c == 7),
                          r=["xTa%d" % s, "xTb%d" % s, "w_in%d" % c], w=[pb])
                if j == 0:
                    P.add("act", f_act(qkv_st[s][:, 0:512], ps[b], AF.Copy, scale=0.125), r=[pb], w=["qst_q%d" % s])
                    if kind == "own":
                        P.add("dve", f_tt(sqt, qkv_st[s][:, 0:512], qkv_st[s][:, 0:512], ALU.mult), r=["qst_q%d" % s],
                              w=["sqt"])
                        P.add("dve", f_red(redq, sqt.rearrange("p (h d) -> p h d", h=8), ALU.add), r=["sqt"], w=["redq"])
                        P.add("dve", f_tt(mxq, mxq, redq, ALU.max), r=["redq", "mxq"], w=["mxq"])
                    if kind == "samp":
                        P.add("act", f_act(sq_f, ps[b], AF.Copy, scale=0.125), r=[pb], w=["sq_f"])
                        P.add("sp", f_dma(SQ[0], sq_f[0:4, :]), r=["sq_f"], w=["SQ0"], dma="sq0")
                elif j == 1:
                    if kind != "halo":
                        P.add("dve", f_copy(kv_f[s][:, 0:512], ps[b]), r=[pb], w=["kvf_k%d" % s])
                    P.add("act", f_copy(qkv_st[s][:, 512:1024], ps[b]), r=[pb], w=["qst_k%d" % s])
                    if kind != "samp":
                        P.add("dve", f_tt(sqt, qkv_st[s][:, 512:1024], qkv_st[s][:, 512:1024], ALU.mult),
                              r=["qst_k%d" % s], w=["sqt"])
                        P.add("dve", f_red(redq, sqt.rearrange("p (h d) -> p h d", h=8), ALU.add), r=["sqt"], w=["redq"])
                        P.add("dve", f_tt(mxk, mxk, redq, ALU.max), r=["redq", "mxk"], w=["mxk"])
                elif j == 2:
                    if kind != "halo":
                        P.add("dve", f_copy(kv_f[s][:, 512:1024], ps[b]), r=[pb], w=["kvf_v%d" % s])
                    va = qkv_st[s][:, 1024:1544].rearrange("p (h d) -> p h d", h=8)
                    P.add("act", f_copy(va[:, :, 0:64], ps[b].rearrange("p (h d) -> p h d", h=8)), r=[pb],
                          w=["qst_v%d" % s])
                    P.add("pool", f_copy(va[:, :, 64:65], vtab[:, tcol:tcol + 1].unsqueeze(1).broadcast_to([128, 8, 1])),
                          r=["vtab"], w=["qst_o%d" % s])
                    lo = 512 if kind == "halo" else 0
                    rr = ["qst_k%d" % s, "qst_v%d" % s, "qst_o%d" % s] + ([] if kind == "halo" else ["qst_q%d" % s])
                    P.add("sp", f_dma(qkv_rows(row0, 1, lo, ROW), qkv_st[s][:, lo:ROW]), r=rr,
                          w=["QKV_%s%d" % (kind, idx)], dma="qkvw%d" % s)
                    if kind == "own":
                        P.add("sp", f_dma(kvp[idx * 128:(idx + 1) * 128, :], kv_f[s]), r=["kvf_k%d" % s, "kvf_v%d" % s],
                              dma="kvpo%d" % s, out=True)
                    elif kind == "samp":
                        P.add("sp", f_dma(kvs, kv_f[s][0:4, :]), r=["kvf_k%d" % s, "kvf_v%d" % s],
                              dma="kvso", out=True)
                        P.add("sp", f_dma(SQ[1], kv_f[s][0:4, 0:512]), r=["kvf_k%d" % s], w=["SQ1"], dma="sq1")
                        P.add("sp", f_dma(SQ[2], kv_f[s][0:4, 512:1024]), r=["kvf_v%d" % s], w=["SQ2"], dma="sq2")
                elif j in (3, 4):
                    psv = ps[b].rearrange("p (h two d) -> p h two d", h=4, two=2)
                    x1 = psv[:, h0:4, 0, :]
                    x2 = psv[:, h0:4, 1, :]
                    cosb = cs_t[s3][:, 0:64].unsqueeze(1).broadcast_to([128, nh, 64])
                    sinb = cs_t[s3][:, 64:128].unsqueeze(1).broadcast_to([128, nh, 64])
                    rv = [t_.rearrange("p (h d) -> p h d", h=4)[:, h0:4, :] for t_ in rt]
                    cn = "cs%d" % s3
                    P.add("dve", f_tt(rv[0], x1, cosb, ALU.mult), r=[pb, cn], w=["rt0"])
                    P.add("dve", f_tt(rv[1], x2, sinb, ALU.mult), r=[pb, cn], w=["rt1"])
                    P.add("dve", f_tt(rv[2], x1, sinb, ALU.mult), r=[pb, cn], w=["rt2"])
                    P.add("dve", f_tt(rv[3], x2, cosb, ALU.mult), r=[pb, cn], w=["rt3"])
                    if kind == "samp":
                        dst = qrot_f if j == 3 else krot_f
                    else:
                        dst = qrot_b if j == 3 else krot_b
                    dname = ("qrot" if j == 3 else "krot") + sfx
                    dv = dst.rearrange("p (h two d) -> p h two d", h=4, two=2)
                    P.add("pool", f_tt(dv[:, h0:4, 0, :], rv[0], rv[1], ALU.subtract), r=["rt0", "rt1"], w=[dname + "a"])
                    P.add("pool", f_tt(dv[:, h0:4, 1, :], rv[2], rv[3], ALU.add), r=["rt2", "rt3"], w=[dname + "b"])
                elif j == 5:
                    if kind == "samp":
                        P.add("act", f_copy(v_f, ps[b]), r=[pb], w=["v_f"])
                    else:
                        P.add("act", f_copy(v_b[:, h0 * 128:512], ps[b][:, h0 * 128:512]), r=[pb], w=["v_b" + sfx])
                elif j == 6:
                    gs = n % 2
                    P.add("act", f_act(g_st[gs], ps[b], AF.Silu), r=[pb], w=["g_st%d" % gs])
                    grow = idx * 128 if kind == "own" else T
                    P.add("sp", f_dma(GS[grow:grow + 128, :], g_st[gs]), r=["g_st%d" % gs], w=["GS%d" % (grow // 128)],
                          dma="gsw%d" % gs)
            if kind == "samp" or "chunks" in enable:
                return None

            def back():
                hsl_ = slice(h0 * 128, 512)
                P.add("pool", f_tt(kdec_b[:, hsl_], krot_b[:, hsl_], kdec[:, hsl_], ALU.mult), r=["krota" + sfx, "krotb" + sfx, "kdec"],
                      w=["kdec_b"])
                if kind == "halo":
                    b3 = nbank("ret")
                    for h in range(h0, 4):
                        hs = slice(h * 128, (h + 1) * 128)
                        P.add("pe", f_mm(ps[b3][:, hs], kdec_b[:, hs], v_b[:, hs], True, True), r=["kdec_b", "v_b" + sfx],
                              w=["ps%d" % b3])
                    for h in range(h0, 4):
                        hs = slice(h * 128, (h + 1) * 128)
                        P.add("dve", f_stt(S_f[:, hs], S_f[:, hs], float(GAM[h] ** 128), ps[b3][:, hs], ALU.mult, ALU.add),
                              r=["ps%d" % b3, "S_f"], w=["S_f"])
                    if idx == NHALO - 1:
                        P.add("act", f_copy(S_b, S_f), r=["S_f"], w=["S_b"])
                    return
                i = idx
                for h in range(4):
                    P.add("pe", f_tr(psb[5][:, h * 128:(h + 1) * 128], qrot_b[:, h * 128:(h + 1) * 128], identb),
                          r=["qrota" + sfx, "qrotb" + sfx, "identb"], w=["ps5"])
                for h in range(4):
                    P.add("pe", f_tr(psb[5][:, 512 + h * 128:512 + (h + 1) * 128], krot_b[:, h * 128:(h + 1) * 128], identb),
                          r=["krota" + sfx, "krotb" + sfx, "identb"], w=["ps5"])
                P.add("act", f_copy(krT, psb[5][:, 512:1024]), r=["ps5"], w=["krT"])
                P.add("dve", f_copy(qrT, psb[5][:, 0:512]), r=["ps5"], w=["qrT"])
                P.add("dve", f_tt(QdecT[:, :, i * 128:(i + 1) * 128], psb[5][:, 0:512].rearrange("p (h t) -> p h t", h=4),
                                  qdec.rearrange("p (h t) -> p h t", h=4), ALU.mult),
                      r=["ps5", "qdec"], w=["QdecT%d" % i])
                b1 = nbank("ret")
                for h in range(4):
                    hs = slice(h * 128, (h + 1) * 128)
                    P.add("pe", f_mm(ps[b1][:, hs], krT[:, hs], qrT[:, hs], True, True), r=["krT", "qrT"], w=["ps%d" % b1])
                P.add("dve", f_tt(sT_b, ps[b1], dmt, ALU.mult), r=["ps%d" % b1, "dmt"], w=["sT_b"])
                b2 = nbank("ret")
                for h in range(4):
                    hs = slice(h * 128, (h + 1) * 128)
                    P.add("pe", f_mm(ps[b2][:, hs], sT_b[:, hs], v_b[:, hs], True, False), r=["sT_b", "v_b" + sfx], w=["ps%d" % b2])
                    P.add("pe", f_mm(ps[b2][:, hs], QdecT[:, h, i * 128:(i + 1) * 128], S_b[:, hs], False, True),
                          r=["QdecT%d" % i, "S_b"], w=["ps%d" % b2])
                os_ = n % 2
                P.add("act", f_copy(o_st[os_], ps[b2]), r=["ps%d" % b2], w=["o_st%d" % os_])
                P.add("sp", f_dma(OL[i * 128:(i + 1) * 128, :], o_st[os_]), r=["o_st%d" % os_], w=["OL%d" % i],
                      dma="olw%d" % os_)
                b3 = nbank("ret")
                for h in range(4):
                    hs = slice(h * 128, (h + 1) * 128)
                    P.add("pe", f_mm(ps[b3][:, hs], kdec_b[:, hs], v_b[:, hs], True, True), r=["kdec_b", "v_b" + sfx],
                          w=["ps%d" % b3])
                for h in range(4):
                    hs = slice(h * 128, (h + 1) * 128)
                    P.add("dve", f_stt(S_f[:, hs], S_f[:, hs], float(GAM[h] ** 128), ps[b3][:, hs], ALU.mult, ALU.add),
                          r=["ps%d" % b3, "S_f"], w=["S_f"])
                P.add("act", f_copy(S_b, S_f), r=["S_f"], w=["S_b"])
            return back
        return front_b

    n = 0
    pend_q = []
    tiles_ = [("halo", i) for i in range(enable.get("halo_start", 0), enable.get("nhalo", NHALO))]
    tiles_ += [("own", i) for i in range(enable.get("nown", 16))]
    if enable.get("samp_tile", True):
        tiles_ += [("samp", 0)]
    NT_ = len(tiles_)
    for k_ in range(min(2, NT_)):
        prefetchA(tiles_[k_][0], tiles_[k_][1], k_)
    fb_q = []
    if NT_:
        fb_q.append(phaseA_tile(tiles_[0][0], tiles_[0][1], 0))
    for ti_ in range(NT_):
        if ti_ + 2 < NT_:
            prefetchA(tiles_[ti_ + 2][0], tiles_[ti_ + 2][1], ti_ + 2)
        if ti_ + 1 < NT_:
            fb_q.append(phaseA_tile(tiles_[ti_ + 1][0], tiles_[ti_ + 1][1], ti_ + 1))
        bk_ = fb_q.pop(0)()
        pend_q.append(bk_)
        if len(pend_q) > 2:
            f_ = pend_q.pop(0)
            if f_ is not None:
                f_()
    for f_ in pend_q:
        if f_ is not None:
            f_()
    if enable.get("stop_at") == "tiles":
        P.emit()
        return nc, P

    P.add("dve", f_red(mfin[:, 0:1], mxq, ALU.max), r=["mxq"], w=["mfin0"])
    P.add("dve", f_red(mfin[:, 1:2], mxk, ALU.max), r=["mxk"], w=["mfin1"])
    P.add("pe", f_tr(ps[6][0:2, 0:128], mfin[:, 0:2], identf), r=["mfin0", "mfin1", "identf"], w=["ps6"])
    P.add("dve", f_red(mfin[0:2, 2:3], ps[6][0:2, 0:128], ALU.max), r=["ps6"], w=["mfin2"])
    P.add("pe", f_tr(ps[7][0:1, 0:2], mfin[0:2, 2:3], identf[0:2, 0:2]), r=["mfin2", "identf"], w=["ps7"])
    P.add("dve", f_copy(mfin[0:1, 5:7], ps[7][0:1, 0:2]), r=["ps7"], w=["mfin5"])
    P.add("dve", f_tt(mfin[0:1, 3:4], mfin[0:1, 5:6], mfin[0:1, 6:7], ALU.mult), r=["mfin5"], w=["mfin3"])
    P.add("act", f_act(mfin[0:1, 3:4], mfin[0:1, 3:4], AF.Sqrt), r=["mfin3"], w=["mfin3"])
    P.add("dve", f_ts(mfin[0:1, 4:5], mfin[0:1, 3:4], -1.001, None, ALU.mult), r=["mfin3"], w=["mfin4"])
    P.add("pe", f_mm(ps[6][:, 0:1], onesf[0:1, :], mfin[0:1, 4:5], True, True), r=["mfin4", "onesf"], w=["ps6"])
    P.add("dve", f_copy(negm, ps[6][:, 0:1]), r=["ps6"], w=["negm"])
    P.add("sp", f_dma(srp.rearrange("h k v -> k h v"), S_f.rearrange("p (h v) -> p h v", h=4)), r=["S_f"],
          dma="srpo", out=True)
    P.fence()

    AX_.reset()
    AY_.reset()
    EB = AY_.get(3 * 8 * 256, BF16).rearrange("p (i h q) -> p i h q", i=3, h=8)
    relb_sb = AX_.get(32, F32)
    oh_sb = AX_.get(1152 + 387, F32)
    fm_sb = AX_.get(1152, F32)
    F_sb = AX_.get(1152, F32)
    bs_sb = AY_.get(387, F32)
    hank = [AX_.get(256, F32) for _ in range(6)]
    P.add("sp", f_dma(relb_sb[0:32, :], relb), w=["relb"], dma="k2_1")
    P.add("sp", f_dma(oh_sb[0:32, :], c_oh), w=["oh"], dma="k2_2")
    P.add("sp", f_dma(fm_sb[0:8, :], c_fmask), w=["fm"], dma="k2_3")
    for i in range(3):
        P.add("pe", f_mm(ps[1 + i][0:32, 0:384], relb_sb[0:32, 0:32], oh_sb[0:32, i * 384:(i + 1) * 384], True, True),
              r=["relb", "oh"], w=["ps%d" % (1 + i)])
        P.add("act", f_act(F_sb[0:8, i * 384:(i + 1) * 384], ps[1 + i][0:8, 0:384], AF.Exp), r=["ps%d" % (1 + i)],
              w=["F_sb%d" % i])
    P.add("pe", f_mm(ps[4][0:32, 0:387], relb_sb[0:32, 0:32], oh_sb[0:32, 1152:1152 + 387], True, True),
          r=["relb", "oh"], w=["ps4"])
    P.add("dve", f_copy(bs_sb[0:32, :], ps[4][0:32, 0:387]), r=["ps4"], w=["bs"])
    P.add("dve", f_tt(F_sb[0:8, :], F_sb[0:8, :], fm_sb[0:8, :], ALU.mult), r=["F_sb0", "F_sb1", "F_sb2", "fm"],
          w=["F_sb"])
    P.add("sp", f_dma(FSC.ap(), F_sb[0:8, :]), r=["F_sb"], w=["FSC"], dma="fsc")
    k_ = 0
    for i in range(3):
        for h in range(8):
            hs = k_ % 6
            pbk = (5, 6, 7)[k_ % 3]
            P.add("sp", f_dma(hank[hs], bass.AP(FSC, h * 1152 + i * 384, [[1, 128], [1, 256]])), r=["FSC"],
                  w=["hank%d" % hs], dma="hank%d" % hs)
            P.add("pe", f_mm(ps[pbk][:, 0:256], jflip, hank[hs], True, True), r=["jflip", "hank%d" % hs],
                  w=["ps%d" % pbk])
            P.add("dve" if k_ % 2 else "act", f_copy(EB[:, i, h, :], ps[pbk][:, 0:256]), r=["ps%d" % pbk],
                  w=["EB"])
            k_ += 1


    if enable.get("samp", True):
        P.fence()
        AX_.reset()
        tapbuf = AX_.get(128 * 64, BF16).rearrange("p (j d) -> p j d", j=128)
        prod = AX_.get(32 * 64, F32).rearrange("p (j d) -> p j d", j=32)
        Lg = AX_.get(387, F32)
        Pp = AX_.get(387, F32)
        q32 = AX_.get(64, F32)
        k32 = AX_.get(64, F32)
        v32 = AX_.get(64, F32)
        opart = AX_.get(13 * 64, F32).rearrange("p (k d) -> p k d", k=13)
        o32 = AX_.get(64, F32)
        o16 = AX_.get(128, F32)
        sm = AX_.get(8, F32)
        st = AX_.get(16 * 128, F32).rearrange("p (g v) -> p g v", g=16)
        Vm = AX_.get(4 * 512, F32).rearrange("p (b n) -> p b n", b=4)
        qTs = AX_.get(16, F32).rearrange("p (h m) -> p h m", h=4)
        qTm = AX_.get(64, F32).rearrange("p (h b m) -> p h b m", h=4, b=4)
        prod2 = AX_.get(512, F32)
        oin = AX_.get(512, F32)
        qk = AX_.get(4, F32)
        S32 = slice(0, 32)
        P.add("sp", f_dma(q32[S32, :], SQ[0].rearrange("b (h d) -> (b h) d", h=8)), r=["SQ0"], w=["q32"], dma="q32")
        P.add("sp", f_dma(k32[S32, :], SQ[1].rearrange("b (h d) -> (b h) d", h=8)), r=["SQ1"], w=["k32"], dma="k32")
        P.add("sp", f_dma(v32[S32, :], SQ[2].rearrange("b (h d) -> (b h) d", h=8)), r=["SQ2"], w=["v32"], dma="v32")

        def load_taps(dil, voff):
            for b in range(4):
                src = bass.AP(cache, b * 2048 * 1024 + (2048 - 128 * dil) * 1024 + voff,
                              [[64, 8], [dil * 1024, 128], [1, 64]])
                P.add("pool", f_dma(tapbuf[b * 8:(b + 1) * 8, :, :], src), w=["tapbuf%d" % b], dma="taps%d" % b)
        for i, dil in enumerate(DILS):
            load_taps(dil, 0)
            for cq in range(4):
                P.add("dve", f_tt(prod[S32], tapbuf[S32, cq * 32:(cq + 1) * 32, :],
                                  q32[S32, :].unsqueeze(1).broadcast_to([32, 32, 64]), ALU.mult),
                      r=["tapbuf0", "tapbuf1", "tapbuf2", "tapbuf3", "q32"], w=["prod"])
                P.add("dve", f_red(Lg[S32, i * 129 + cq * 32:i * 129 + (cq + 1) * 32], prod[S32], ALU.add),
                      r=["prod"], w=["Lg"])
        P.add("dve", f_tt(prod[S32, 0, :], k32[S32, :], q32[S32, :], ALU.mult), r=["k32", "q32"], w=["prod"])
        P.add("dve", f_red(sm[S32, 0:1], prod[S32, 0, :], ALU.add), r=["prod"], w=["sm0"])
        for i in range(3):
            P.add("dve", f_copy(Lg[S32, i * 129 + 128:i * 129 + 129], sm[S32, 0:1]), r=["sm0"], w=["Lg"])
        P.add("dve", f_tt(Lg[S32], Lg[S32], bs_sb[S32], ALU.add), r=["Lg", "bs"], w=["Lg"])
        P.add("dve", f_red(sm[S32, 1:2], Lg[S32], ALU.max), r=["Lg"], w=["sm1"])
        P.add("dve", f_ts(sm[S32, 2:3], sm[S32, 1:2], -1.0, None, ALU.mult), r=["sm1"], w=["sm2"])
        P.add("dve", f_memset(sm[S32, 3:4], 0.0), w=["sm3"])
        P.add("act", f_act(Pp[S32], Lg[S32], AF.Exp, bias=sm[S32, 2:3], scale=1.0, accum_out=sm[S32, 3:4]),
              r=["Lg", "sm2", "sm3"], w=["Pp", "sm3"])
        for i, dil in enumerate(DILS):
            load_taps(dil, 512)
            for cq in range(4):
                P.add("dve", f_tt(prod[S32], tapbuf[S32, cq * 32:(cq + 1) * 32, :],
                                  Pp[S32, i * 129 + cq * 32:i * 129 + (cq + 1) * 32].unsqueeze(2).broadcast_to([32, 32, 64]),
                                  ALU.mult), r=["tapbuf0", "tapbuf1", "tapbuf2", "tapbuf3", "Pp"], w=["prod"])
                P.add("dve", f_red(opart[S32, i * 4 + cq, :], prod[S32].rearrange("p j d -> p d j"), ALU.add),
                      r=["prod"], w=["opart"])
        P.add("dve", f_tt(sm[S32, 4:5], Pp[S32, 128:129], Pp[S32, 257:258], ALU.add), r=["Pp"], w=["sm4"])
        P.add("dve", f_tt(sm[S32, 4:5], sm[S32, 4:5], Pp[S32, 386:387], ALU.add), r=["Pp", "sm4"], w=["sm4"])
        P.add("dve", f_ts(opart[S32, 12, :], v32[S32, :], sm[S32, 4:5], None, ALU.mult), r=["v32", "sm4"],
              w=["opart"])
        P.add("dve", f_red(o32[S32], opart[S32].rearrange("p k d -> p d k"), ALU.add), r=["opart"], w=["o32"])
        P.add("dve", f_recip(sm[S32, 5:6], sm[S32, 3:4]), r=["sm3"], w=["sm5"])
        P.add("dve", f_ts(o32[S32], o32[S32], sm[S32, 5:6], None, ALU.mult), r=["o32", "sm5"], w=["o32"])
        P.add("sp", f_dma(SATT, o32[S32, :]), r=["o32"], w=["SATT"], dma="satt")
        P.add("sp", f_dma(o16[0:16, :], SATT.rearrange("(g par) d -> g (par d)", par=2)), r=["SATT"], w=["o16"],
              dma="o16")
        P.add("pe", f_tr(ps[1][:, 0:16], o16[0:16, :], identf[0:16, 0:16]), r=["o16", "identf"], w=["ps1"])
        P.add("act", f_copy(att_s[:, 0:4, 0:4], ps[1][:, 0:16].rearrange("p (b q) -> p q b", b=4)), r=["ps1", "att_s"],
              w=["att_s"])
        P.add("pool", f_ts(krot_f, krot_f, float(128.0 ** -0.5), None, ALU.mult), r=["krota", "krotb"],
              w=["krota", "krotb"])
        P.add("sp", f_dma(st, state.rearrange("b h k v -> k (b h) v")), w=["st"], dma="st")
        for b in range(4):
            P.add("dve", f_ts(Vm[0:4, b, :], v_f[0:4, :], delta[0:4, b:b + 1], None, ALU.mult), r=["v_f", "delta"],
                  w=["Vm"])
        for h in range(4):
            P.add("pe", f_tr(ps[2][:, h * 128:(h + 1) * 128], qrot_f[:, h * 128:(h + 1) * 128], identf),
                  r=["qrota", "qrotb", "identf"], w=["ps2"])
        P.add("act", f_copy(qTs, ps[2].rearrange("p (h t) -> p h t", h=4)[:, :, 0:4]), r=["ps2"], w=["qTs"])
        for b in range(4):
            P.add("dve", f_tt(qTm[:, :, b, :], qTs, drow[:, b * 4:(b + 1) * 4].unsqueeze(1).broadcast_to([128, 4, 4]),
                              ALU.mult), r=["qTs", "drow"], w=["qTm"])
        for h in range(4):
            for b in range(4):
                P.add("pe", f_mm(ps[3][0:4, h * 128:(h + 1) * 128], qTm[:, h, b, :], st[:, b * 4 + h, :], b == 0,
                                 b == 3), r=["qTm", "st"], w=["ps3"])
        P.add("dve", f_tt(prod2[0:4, :], qrot_f[0:4, :], krot_f[0:4, :], ALU.mult), r=["qrota", "qrotb", "krota", "krotb"],
              w=["prod2"])
        P.add("dve", f_red(qk[0:4, :], prod2[0:4, :].rearrange("p (h d) -> p h d", h=4), ALU.add), r=["prod2"],
              w=["qk"])
        P.add("dve", f_tt(oin[0:4, :].rearrange("p (h d) -> p h d", h=4),
                          v_f[0:4, :].rearrange("p (h d) -> p h d", h=4),
                          qk[0:4, :].unsqueeze(2).broadcast_to([4, 4, 128]), ALU.mult), r=["v_f", "qk"], w=["oin"])
        P.add("dve", f_tt(oret_s[0:4, :], ps[3][0:4, :], gam[0:4, :], ALU.mult), r=["ps3", "gam", "oret_s"],
              w=["oret_s"])
        P.add("dve", f_tt(oret_s[0:4, :], oret_s[0:4, :], oin[0:4, :], ALU.add), r=["oret_s", "oin"], w=["oret_s"])
        for b in range(4):
            bk = 4 + b
            for h in range(4):
                hs = slice(h * 128, (h + 1) * 128)
                P.add("pe", f_mm(ps[bk][:, hs], krot_f[0:4, hs], Vm[0:4, b, hs], True, True),
                      r=["krota", "krotb", "Vm"], w=["ps%d" % bk])
            for h in range(4):
                hs = slice(h * 128, (h + 1) * 128)
                P.add("dve", f_stt(st[:, b * 4 + h, :], st[:, b * 4 + h, :], float(GAM[h]), ps[bk][:, hs], ALU.mult,
                                   ALU.add), r=["ps%d" % bk, "st"], w=["st"])
        P.add("sp", f_dma(srs.rearrange("b h k v -> k (b h) v"), st), r=["st"], dma="srso", out=True)

    P.fence()
    if enable.get("stop_after_A", False):
        P.emit()
        return nc, P

    AX_.reset()
    AX_.size = 65536
    Oacc_full = AX_.get(8 * 2048, F32)
    Oacc = Oacc_full.rearrange("p (h t) -> p h t", h=8)
    kvt = [AY_.get(1032, BF16) for _ in range(8)]
    qt = [AY_.get(512, BF16) for _ in range(2)]
    KT = [AY_.get(512, BF16) for _ in range(8)]
    QTe = [AY_.get(512, BF16) for _ in range(2)]
    QTo = [AY_.get(512, BF16) for _ in range(2)]
    for q_ in range(2):
        P.add("pool", f_memset(QTe[q_], 0.0), w=["QT%d" % q_])
        P.add("pool", f_memset(QTo[q_], 0.0), w=["QT%d" % q_])
    P.add("pool", f_memset(Oacc_full, 0.0), w=["Oacc0", "Oacc1"])
    Pexp = [AY_.get(2048, BF16) for _ in range(2)]
    PTm = [AY_.get(2048, BF16) for _ in range(2)]
    cnt = {"kv": 0, "q": 0, "blk": 0, "tr": 0}

    def load_kv(bi, dil, r, m):
        s = cnt["kv"] % 8
        cnt["kv"] += 1
        base = 2048 + m * 128 * dil + r
        P.add("sp", f_dma(kvt[s], qkv_rows(base, dil, 512, ROW)), r=["QKVall"], w=["kvt%d" % s], dma="kvt%d" % s)
        tb = (0, 5)[cnt["tr"] % 2]
        cnt["tr"] += 1
        for p_ in range(4):
            P.add("pe", f_tr(psb[tb][:, p_ * 128:(p_ + 1) * 128], kvt[s][:, p_ * 128:(p_ + 1) * 128], identb),
                  r=["kvt%d" % s, "identb"], w=["ps%d" % tb])
        P.add("act", f_copy(KT[s], psb[tb][:, 0:512]), r=["ps%d" % tb], w=["KT%d" % s])
        return s

    BL = enable.get("B_lvl", 9)
    def stage1a(bi, dil, r, m):
        qs = cnt["q"] % 2
        cnt["q"] += 1
        base = 2048 + m * 128 * dil + r
        P.add("sp", f_dma(qt[qs], qkv_rows(base, dil, 0, 512)), r=["QKVall"], w=["qt%d" % qs], dma="qt%d" % qs)
        tb = (0, 5)[cnt["tr"] % 2]
        cnt["tr"] += 1
        for p_ in range(4):
            P.add("pe", f_tr(psb[tb][:, p_ * 128:(p_ + 1) * 128], qt[qs][:, p_ * 128:(p_ + 1) * 128], identb),
                  r=["qt%d" % qs, "identb"], w=["ps%d" % tb])
        P.add("dve", f_copy(QTe[qs][0:64, :], psb[tb][0:64, 0:512]), r=["ps%d" % tb], w=["QT%d" % qs])
        P.add("dve", f_copy(QTo[qs][64:128, :], psb[tb][64:128, 0:512]), r=["ps%d" % tb], w=["QT%d" % qs])
        return qs

    def stage1b(info):
        bi, dil, r, m, scur, sprev, qs = info
        bs_ = cnt["blk"] % 2
        cnt["blk"] += 1
        for half in range(2):
            banks = (1, 2) if half == 0 else (3, 4)
            for hh in range(2):
                bk = banks[hh]
                for h2 in range(2):
                    h = half * 4 + hh * 2 + h2
                    cs_ = slice((h // 2) * 128, (h // 2) * 128 + 128)
                    QTm_ = (QTe if h % 2 == 0 else QTo)[qs]
                    P.add("pe", f_mm(ps[bk][:, h2 * 256:h2 * 256 + 128], KT[scur][:, cs_], QTm_[:, cs_],
                                     True, True),
                          r=["KT%d" % scur, "QT%d" % qs], w=["ps%d" % bk])
                    P.add("pe", f_mm(ps[bk][:, h2 * 256 + 128:h2 * 256 + 256], KT[sprev][:, cs_],
                                     QTm_[:, cs_], True, True),
                          r=["KT%d" % sprev, "QT%d" % qs], w=["ps%d" % bk])
                hb = half * 4 + hh * 2
                P.add("act", f_act(Pexp[bs_][:, hb * 256:(hb + 2) * 256], ps[bk], AF.Exp, bias=negm[:, 0:1],
                                   scale=1.0),
                      r=["ps%d" % bk, "negm"], w=["Pexp%d_%d" % (bs_, half)])
            hsl = slice(half * 1024, (half + 1) * 1024)
            P.add("dve", f_tt(PTm[bs_][:, hsl], Pexp[bs_][:, hsl],
                              EB[:, bi, half * 4:(half + 1) * 4, :].rearrange("p h q -> p (h q)"), ALU.mult),
                  r=["Pexp%d_%d" % (bs_, half), "EB"], w=["PTm%d_%d" % (bs_, half)])
        return (bi, dil, r, m, scur, sprev, bs_)

    def stage2(st2):
        bi, dil, r, m, scur, sprev, bs_ = st2
        for half in range(2):
            ob = 6 + half
            for h4 in range(4):
                h = half * 4 + h4
                vcur = kvt[scur][:, 512 + h * 65:512 + (h + 1) * 65]
                vprev = kvt[sprev][:, 512 + h * 65:512 + (h + 1) * 65]
                pc = PTm[bs_][:, h * 256:h * 256 + 128]
                pp = PTm[bs_][:, h * 256 + 128:h * 256 + 256]
                o_ = ps[ob][0:65, h4 * 128:(h4 + 1) * 128]
                P.add("pe", f_mm(o_, vprev, pp, True, False), r=["kvt%d" % sprev, "PTm%d_%d" % (bs_, half)],
                      w=["ps%d" % ob])
                P.add("pe", f_mm(o_, vcur, pc, False, True), r=["kvt%d" % scur, "PTm%d_%d" % (bs_, half)],
                      w=["ps%d" % ob])
            t0 = m * 128 * dil + r
            osl = Oacc[0:65, half * 4:(half + 1) * 4, t0:t0 + 127 * dil + 1:dil]
            pin = ps[ob][0:65, :].rearrange("p (h t) -> p h t", h=4)
            if bi == 0:
                P.add("act", f_copy(osl, pin), r=["ps%d" % ob], w=["Oacc%d" % half])
            else:
                P.add("dve", f_tt(osl, osl, pin, ALU.add), r=["ps%d" % ob, "Oacc%d" % half],
                      w=["Oacc%d" % half])

    pendA = None
    pend2 = None
    sprev = None
    for bi, dil in enumerate(DILS[:enable.get("B_nbr", 3)]):
        nblk = 16 // dil
        for r in range(dil):
            for m in range(min(nblk, enable.get("B_nm", 99))):
                if m == 0:
                    sprev = load_kv(bi, dil, r, -1)
                scur = load_kv(bi, dil, r, m)
                qs = stage1a(bi, dil, r, m)
                infoA = (bi, dil, r, m, scur, sprev, qs)
                if pendA is not None:
                    cur2 = stage1b(pendA)
                    if pend2 is not None:
                        stage2(pend2)
                    pend2 = cur2
                pendA = infoA
                sprev = scur
    if pendA is not None:
        cur2 = stage1b(pendA)
        if pend2 is not None:
            stage2(pend2)
        pend2 = cur2
    if pend2 is not None:
        stage2(pend2)
    if enable.get("stop_at") == "B":
        P.emit()
        return nc, P
    att_st = [Pexp[1]] * 2
    rcp2 = [Pexp[0][:, 0:1024].bitcast(F32), Pexp[0][:, 1024:2048].bitcast(F32)]
    for h in range(8):
        as_ = 0
        for q4 in range(4):
            tsl = slice(q4 * 512, (q4 + 1) * 512)
            bk = 1 + (h * 4 + q4) % 4
            P.add("pe", f_mm(ps[bk][0:64, :], e64[:, 0:64], Oacc[:, h, tsl], True, True),
                  r=["Oacc0", "Oacc1", "e64"], w=["ps%d" % bk])
            rk = (h * 4 + q4) % 2
            rcp = rcp2[rk]
            P.add("dve", f_recip(rcp[0:64, :], ps[bk][0:64, :]), r=["ps%d" % bk, "Pexp0_0", "Pexp0_1", "PTm0_0", "PTm0_1"],
                  w=["rcp%d" % rk])
            P.add("pool", f_tt(att_st[as_][0:64, tsl], Oacc[0:64, h, tsl], rcp[0:64, :], ALU.mult),
                  r=["rcp%d" % rk, "Oacc0", "Oacc1", "Pexp1_0", "Pexp1_1", "PTm1_0", "PTm1_1"], w=["att_st%d_%d" % (as_, q4)])
        P.add("sp", f_dma(ATT[:, h * T:(h + 1) * T], att_st[as_][0:64, :]), r=["att_st%d_%d" % (as_, q) for q in range(4)],
              w=["ATT"], dma="attw%d" % as_)
    P.fence()

    if enable.get("stop_at") == "norm":
        P.emit()
        return nc, P
    AX_.reset()
    AY_.reset()
    x1T = AX_.get(17 * 1024, BF16).rearrange("p (t c k) -> p t c k", t=17, c=8)
    wup = [AX_.get(8 * 256, BF16).rearrange("p (c f) -> p c f", c=8) for _ in range(2)]
    wdn = [AX_.get(2 * 1024, BF16).rearrange("p (c n) -> p c n", c=2) for _ in range(2)]
    hT = AX_.get(2 * 2176, BF16).rearrange("p (c t) -> p c t", c=2)
    yacc = AY_.get(17 * 1024, F32).rearrange("p (t n) -> p t n", t=17)
    att_t2 = [sb("att_t%d" % i, 512, BF16).rearrange("p (c t) -> p c t", c=4) for i in range(2)]
    xf2 = [sb("xf%d" % i, 1024, F32) for i in range(2)]
    rr_ = sb("rr", 1024, F32)
    x1b = sb("x1b", 1024, BF16)
    gt2 = [sb("gt%d" % i, 512, BF16) for i in range(2)]
    ot2 = [sb("ot%d" % i, 512, F32) for i in range(2)]
    yrb2 = [sb("yrb%d" % i, 512, BF16) for i in range(2)]
    relu_t = [AX_.get(512, F32) for i in range(2)]
    for c in range(8):
        P.add("pool", f_dma(w_out_sb[:, c, :], w_out[c * 128:(c + 1) * 128, :]), w=["w_out"], dma="w_out")
    P.add("sp", f_dma(lng, bass.AP(ln1g, 0, [[0, 128], [1, 1024]])), w=["lng"], dma="lng")
    P.add("sp", f_dma(lnb, bass.AP(ln1b, 0, [[0, 128], [1, 1024]])), w=["lnb"], dma="lnb")

    def layer_norm(src, dst, eps):
        P.add("dve", f_bnstats(stats[:, 0:6], src[:, 0:512]), r=[src_name[0]], w=["stats"])
        P.add("dve", f_bnstats(stats[:, 6:12], src[:, 512:1024]), r=[src_name[0]], w=["stats"])
        P.add("dve", f_bnaggr(stats[:, 12:14], stats[:, 0:12]), r=["stats"], w=["stats"])
        P.add("dve", f_ts(stats[:, 14:15], stats[:, 13:14], eps, None, ALU.add), r=["stats"], w=["stats"])
        P.add("act", f_act(stats[:, 14:15], stats[:, 14:15], AF.Sqrt), r=["stats"], w=["stats"])
        P.add("dve", f_recip(stats[:, 14:15], stats[:, 14:15]), r=["stats"], w=["stats"])
        P.add("dve", f_stt(stats[:, 15:16], stats[:, 12:13], -1.0, stats[:, 14:15], ALU.mult, ALU.mult),
              r=["stats"], w=["stats"])
        P.add("dve", f_ts(dst, src, stats[:, 14:15], stats[:, 15:16], ALU.mult, ALU.add),
              r=["stats", src_name[0]], w=[src_name[1]])
        P.add(aff_eng[0], f_tt(dst, dst, lng, ALU.mult), r=[src_name[1], "lng"], w=[src_name[1]])
        P.add(aff_eng[1], f_tt(dst, dst, lnb, ALU.add), r=[src_name[1], "lnb"], w=[src_name[1]])

    src_name = ["rr", "rr"]
    aff_eng = ["dve", "dve"]
    attv = ATT.rearrange("d (h t) -> d h t", h=8)

    def prefetchD(ti):
        own = ti < 16
        sl = ti % 2
        if own:
            P.add("sp", f_dma(ot2[sl], OL[ti * 128:(ti + 1) * 128, :]), r=["OL%d" % ti], w=["ot%d" % sl], dma="olt%d" % sl)
            for par in range(2):
                P.add("sp", f_dma(att_t2[sl][par * 64:(par + 1) * 64, :, :], attv[:, par:8:2, ti * 128:(ti + 1) * 128]),
                      r=["ATT"], w=["att_t%d_%d" % (sl, par)], dma="att_t%d_%d" % (sl, par))
        grow = ti * 128 if own else T
        P.add("sp", f_dma(gt2[sl], GS[grow:grow + 128, :]), r=["GS%d" % (grow // 128)], w=["gt%d" % sl], dma="gt%d" % sl)
        xsrc = xo[ti * 128:(ti + 1) * 128, :] if own else xs
        P.add("sp", f_dma(xf2[sl], xsrc), w=["xf%d" % sl], dma="xf%d" % sl)

    def stageD1(ti):
        own = ti < 16
        sl = ti % 2
        gt = gt2[sl]
        yrb = yrb2[sl]
        if own:
            osrc = ot2[sl]
            oname = "ot%d" % sl
        else:
            osrc = oret_s
            oname = "oret_s"
        for h in range(4):
            hs = slice(h * 128, (h + 1) * 128)
            P.add("dve", f_bnstats(stats[:, 16 + h * 6:22 + h * 6], osrc[:, hs]), r=[oname], w=["gstats"])
            P.add("dve", f_bnaggr(stats[:, 40 + 2 * h:42 + 2 * h], stats[:, 16 + h * 6:22 + h * 6]), r=["gstats"],
                  w=["gstats"])
        P.add("dve", f_ts(stats[:, 48:52], stats[:, 41:49:2], 1e-6, None, ALU.add), r=["gstats"], w=["gstats"])
        P.add("act", f_act(stats[:, 48:52], stats[:, 48:52], AF.Sqrt), r=["gstats"], w=["gstats"])
        P.add("dve", f_recip(stats[:, 48:52], stats[:, 48:52]), r=["gstats"], w=["gstats"])
        P.add("dve", f_stt(stats[:, 52:56], stats[:, 40:48:2], -1.0, stats[:, 48:52], ALU.mult, ALU.mult),
              r=["gstats"], w=["gstats"])
        for h in range(4):
            hs = slice(h * 128, (h + 1) * 128)
            P.add("dve", f_ts(osrc[:, hs], osrc[:, hs], stats[:, 48 + h:49 + h], stats[:, 52 + h:53 + h],
                              ALU.mult, ALU.add), r=["gstats", oname], w=[oname])
        P.add("pool", f_tt(yrb, osrc, gt, ALU.mult), r=[oname, "gt%d" % sl], w=["yrb%d" % sl])
        for h in range(4):
            P.add("pe", f_tr(psb[5][:, h * 128:(h + 1) * 128], yrb[:, h * 128:(h + 1) * 128], identb),
                  r=["yrb%d" % sl, "identb"], w=["ps5"])
        P.add("act", f_copy(QdecT[:, :, ti * 128:(ti + 1) * 128],
                            psb[5][:, 0:512].rearrange("p (h t) -> p h t", h=4)), r=["ps5"], w=["QdecT%d" % ti])

    x1b2 = [x1b, S_f.bitcast(BF16)]

    def stageD2b(ti):
        sl_ = ti % 2
        xb_ = x1b2[sl_]
        for c in range(8):
            P.add("pe", f_tr(psb[0][:, c * 128:(c + 1) * 128], xb_[:, c * 128:(c + 1) * 128], identb),
                  r=["x1b%d" % sl_, "identb"], w=["ps0"])
        P.add("act", f_copy(x1T[:, ti, :, :].rearrange("p c k -> p (c k)"), psb[0]), r=["ps0"], w=["x1T"])

    prefetchD(0)
    prefetchD(1)
    stageD1(0)
    for ti in range(17):
        own = ti < 16
        sl = ti % 2
        if ti + 2 < 17:
            pass
        if ti + 1 < 17:
            stageD1(ti + 1)
        xf = xf2[sl]
        if own:
            a_t = att_t2[sl]
            anames = ["att_t%d_0" % sl, "att_t%d_1" % sl]
        else:
            a_t = att_s
            anames = ["att_s"]
        mb = (1, 2) if ti % 2 == 0 else (3, 4)
        for half in range(2):
            nsl = slice(half * 512, (half + 1) * 512)
            for c in range(4):
                P.add("pe", f_mm(ps[mb[half]], a_t[:, c, :], w_out_sb[:, c, nsl], c == 0, False),
                      r=anames + ["w_out"], w=["ps%d" % mb[half]])
            for c in range(4):
                P.add("pe", f_mm(ps[mb[half]], QdecT[:, c, ti * 128:(ti + 1) * 128], w_out_sb[:, 4 + c, nsl], False,
                                 c == 3),
                      r=["QdecT%d" % ti, "w_out"], w=["ps%d" % mb[half]])
            P.add("dve", f_stt(rr_[:, nsl], xf[:, nsl], float(ALPHA), ps[mb[half]], ALU.mult, ALU.add),
                  r=["xf%d" % sl, "ps%d" % mb[half]], w=["rr"])
        layer_norm(rr_, rr_, 1e-5)
        P.add("act", f_act(yacc[:, ti, :], rr_, AF.Copy, scale=float(ALPHA)), r=["rr"], w=["yacc%d" % ti])
        x1b_ = x1b2[sl]
        P.add("act", f_copy(x1b_, rr_), r=["rr"], w=["x1b%d" % sl])
        if ti >= 1:
            stageD2b(ti - 1)
        if ti + 2 < 17:
            prefetchD(ti + 2)
    stageD2b(16)

    if enable.get("stop_at") == "D":
        P.emit()
        return nc, P
    aff_eng[0] = "pool"
    aff_eng[1] = "dve"
    P.add("sp", f_dma(lng, bass.AP(ln2g, 0, [[0, 128], [1, 1024]])), w=["lng"], dma="lng")
    P.add("sp", f_dma(lnb, bass.AP(ln2b, 0, [[0, 128], [1, 1024]])), w=["lnb"], dma="lnb")
    groups = [(0, 4), (4, 8), (8, 12), (12, 16), (16, 17)]
    w_up_v = w_up.rearrange("(c p) f -> p c f", p=128)
    hcnt = 0
    ycnt = 0
    for fb in range(16):
        s = fb % 2
        P.add("pool", f_dma(wup[s], w_up_v[:, :, fb * 256:(fb + 1) * 256]), w=["wup%d" % s], dma="wup%d" % s)
        P.add("pool", f_dma(wdn[s], w_down[fb * 256:(fb + 1) * 256, :].rearrange("(c p) n -> p c n", p=128)),
              w=["wdn%d" % s], dma="wdn%d" % s)
        for fc in range(2):
            for (g0, g1) in groups:
                ntok = (g1 - g0) * 128
                hb = (1, 2, 3)[hcnt % 3]
                rs = hcnt % 2
                hcnt += 1
                for c in range(8):
                    P.add("pe", f_mm(ps[hb][:, 0:ntok], wup[s][:, c, fc * 128:(fc + 1) * 128],
                                     x1T[:, g0:g1, c, :], c == 0, c == 7),
                          r=["wup%d" % s, "x1T"], w=["ps%d" % hb])
                P.add("act", f_act(relu_t[rs][:, 0:ntok], ps[hb][:, 0:ntok], AF.Relu), r=["ps%d" % hb],
                      w=["relu%d" % rs])
                P.add("pool", f_tt(hT[:, fc, g0 * 128:g1 * 128], relu_t[rs][:, 0:ntok], relu_t[rs][:, 0:ntok],
                                   ALU.mult), r=["relu%d" % rs], w=["hT%d_%d" % (fc, g0)])
        for ti in range(17):
            g0 = [g for g in groups if g[0] <= ti < g[1]][0][0]
            for half in range(2):
                nsl = slice(half * 512, (half + 1) * 512)
                yb = (4, 5, 6, 7)[ycnt % 4]
                ycnt += 1
                for fc in range(2):
                    P.add("pe", f_mm(ps[yb], hT[:, fc, ti * 128:(ti + 1) * 128], wdn[s][:, fc, nsl], fc == 0, fc == 1),
                          r=["hT%d_%d" % (fc, g0), "wdn%d" % s], w=["ps%d" % yb])
                P.add("dve", f_tt(yacc[:, ti, nsl], yacc[:, ti, nsl], ps[yb], ALU.add),
                      r=["ps%d" % yb, "yacc%d" % ti], w=["yacc%d" % ti])
            if fb == 15:
                src_name[0] = "yacc%d" % ti
                src_name[1] = "yacc%d" % ti
                layer_norm(yacc[:, ti, :], yacc[:, ti, :], 1e-5)
                if ti < 16:
                    P.add("sp", f_dma(y_p[ti * 128:(ti + 1) * 128, :], yacc[:, ti, :]), r=["yacc%d" % ti], dma="ypo",
                          out=True)
                else:
                    P.add("sp", f_dma(y_s, yacc[0:4, ti, :]), r=["yacc%d" % ti], dma="yso", out=True)
    P.emit()
    return nc, P


def _t5_bucket(dist):
    dist = np.asarray(dist)
    d_f = np.maximum(dist, 1).astype(np.float32)
    large = 16 + (np.log(d_f / np.float32(16)) / np.float32(math.log(2048 / 16)) * np.float32(16)).astype(np.int32)
    large = np.minimum(large, 31)
    return np.where(dist < 16, dist, large)


def _constants(core):
    c = {}
    c["c_ident"] = np.eye(128, dtype=np.float32)
    c["c_jflip"] = np.ascontiguousarray(np.eye(128, dtype=np.float32)[::-1])
    half = 64
    inv_freq = (np.float32(1.0) / (np.float32(10000.0) ** np.linspace(0.0, 1.0, half, dtype=np.float32))).astype(np.float32)
    cs = np.zeros((NHALO + 17, 128, 2, 64), np.float32)
    for ti in range(NHALO + 17):
        if ti < NHALO + 16:
            pos = (core * T - NHALO * 128 + ti * 128 + np.arange(128)).astype(np.float32)
        else:
            pos = np.full(128, 16384, np.float32)
        ang = (pos[:, None] * inv_freq[None, :]).astype(np.float32)
        cs[ti, :, 0, :] = np.cos(ang)
        cs[ti, :, 1, :] = np.sin(ang)
    c["c_cs"] = cs.reshape((NHALO + 17) * 128, 128)
    vt = np.ones((128, 33), np.float32)
    if core == 0:
        vt[:, 0:16] = 0.0
    c["c_vtab"] = vt
    lg = np.log(np.array(GAM, np.float64))
    kk = np.arange(128)[:, None]
    qq = np.arange(128)[None, :]
    dmt = np.zeros((128, 4, 128), np.float64)
    qdec = np.zeros((128, 4, 128), np.float64)
    kdec = np.zeros((128, 4, 128), np.float64)
    gam = np.zeros((128, 4, 128), np.float64)
    sc = 128.0 ** -0.5
    for h in range(4):
        dmt[:, h, :] = np.where(qq >= kk, np.exp(lg[h] * np.maximum(qq - kk, 0)), 0.0) * sc
        qdec[:, h, :] = np.exp(lg[h] * (np.arange(128) + 1.0))[None, :]
        kdec[:, h, :] = (np.exp(lg[h] * (127.0 - np.arange(128))) * sc)[:, None]
        gam[:, h, :] = GAM[h]
    c["c_dmt"] = dmt.reshape(128, 512).astype(np.float32)
    c["c_qdec"] = qdec.reshape(128, 512).astype(np.float32)
    c["c_kdec"] = kdec.reshape(128, 512).astype(np.float32)
    c["c_gam"] = gam.reshape(128, 512).astype(np.float32)
    oh = np.zeros((32, 1152 + 387), np.float32)
    fm = np.zeros((8, 1152), np.float32)
    for i, dil in enumerate(DILS):
        for idx in range(384):
            rel = idx - 127
            if 0 <= rel <= 128:
                oh[int(_t5_bucket(rel * dil)), i * 384 + idx] = 1.0
                fm[:, i * 384 + idx] = 1.0
        for jj in range(129):
            tap = 128 - jj if jj < 128 else 0
            oh[int(_t5_bucket(tap * dil)), 1152 + i * 129 + jj] = 1.0
    c["c_oh"] = oh
    c["c_fmask"] = fm
    d = np.zeros((128, 4), np.float32)
    d[0:4, 0:4] = np.eye(4)
    c["c_delta"] = d
    dr = np.zeros((128, 4, 4), np.float32)
    for b in range(4):
        dr[:, b, b] = 1.0
    c["c_drow"] = dr.reshape(128, 16)
    return c


ENABLE = {"samp": True}
_CACHE = {}


def kernel(x_prompt, x_sample, cache_kv_win, state_ret, w_in, rel_bias, w_out,
           ln1_g, ln1_b, w_up, w_down, ln2_g, ln2_b):
    f = lambda a: np.ascontiguousarray(np.asarray(a, dtype=np.float32))
    x_prompt, x_sample, cache_kv_win, state_ret = f(x_prompt), f(x_sample), f(cache_kv_win), f(state_ret)
    if "nc" not in _CACHE:
        _CACHE["nc"] = build(ENABLE)
    nc, P = _CACHE["nc"]
    xp = x_prompt[0]
    in_maps = []
    for c in range(NCORES):
        m = {}
        m["xo"] = xp[c * T:(c + 1) * T]
        xh_ = np.zeros((NHALO * 128, 1024), np.float32)
        lo = c * T - NHALO * 128
        if c > 0:
            xh_[max(0, -lo):] = xp[max(lo, 0):c * T]
        m["xh"] = xh_
        xs = np.zeros((128, 1024), np.float32)
        xs[0:4] = x_sample[c * 4:(c + 1) * 4, 0]
        m["xs"] = xs
        m["cache"] = cache_kv_win[0, c * 4:(c + 1) * 4].reshape(4, 2048, 1024)
        m["state"] = state_ret[0, c * 4:(c + 1) * 4]
        m["w_in"] = f(w_in)[0]
        m["relb"] = np.ascontiguousarray(np.tile(f(rel_bias), (1, 4)))
        m["w_out"] = f(w_out)[0]
        m["ln1g"] = f(ln1_g)
        m["ln1b"] = f(ln1_b)
        m["w_up"] = f(w_up)[0]
        m["w_down"] = f(w_down)[0]
        m["ln2g"] = f(ln2_g)
        m["ln2b"] = f(ln2_b)
        m.update(_constants(c))
        in_maps.append({k: np.ascontiguousarray(v) for k, v in m.items()})
    res = run_bass_kernel_spmd(nc, in_maps, core_ids=list(range(NCORES)))
    R = res.results
    y_prompt = np.concatenate([R[c]["y_p"] for c in range(NCORES)], axis=0)[None]
    y_sample = np.concatenate([R[c]["y_s"] for c in range(NCORES)], axis=0)[:, None, :]
    kv_win_prompt = R[7]["kvp"].reshape(1, 1, 2048, 2, 8, 64)
    kv_win_sample = np.concatenate([R[c]["kvs"] for c in range(NCORES)], axis=0).reshape(1, 32, 1, 2, 8, 64)
    state_ret_prompt = R[7]["srp"].reshape(1, 1, 4, 128, 128)
    state_ret_sample = np.concatenate([R[c]["srs"] for c in range(NCORES)], axis=0)[None]
    return (y_prompt.astype(np.float32), y_sample.astype(np.float32), kv_win_prompt.astype(np.float32),
            kv_win_sample.astype(np.float32), state_ret_prompt.astype(np.float32),
            state_ret_sample.astype(np.float32))
```

```python
import math
import numpy as np
import concourse.bass as bass
import concourse.mybir as mybir
from concourse.bass_utils import run_bass_kernel_spmd

F32 = mybir.dt.float32
BF16 = mybir.dt.bfloat16
ALU = mybir.AluOpType
AF = mybir.ActivationFunctionType
AX = mybir.AxisListType

SAME_ENGINE_SYNC = True
NCORES = 8
NHALO = 42
HJ = (6, 11, 21, 42)
T = 2048
ALPHA = 2.0 ** 0.25
ROW = 1544
DILS = (1, 4, 16)
GAM = [1.0 - 2.0 ** (-5.0 - h) for h in range(4)]


class Op:
    __slots__ = ("eng", "fn", "deps", "is_dma", "key", "signal", "cnt", "out", "desc")


class Prog:
    def __init__(self, nc, fence_tile):
        self.nc = nc
        self.ops = []
        self.last_w = {}
        self.readers = {}
        self.eng = {"pe": nc.tensor, "act": nc.scalar, "dve": nc.vector,
                    "pool": nc.gpsimd, "sp": nc.sync}
        self.out_dmas = []
        self.last_op = {}
        self.dmas_since = []
        self.fence_op = None
        self.synced = set()
        self.fence_tile = fence_tile
        self.dma_chain = {"pool": 6}
        self.dma_hist = {}

    def _mk(self, eng, fn, dma, out):
        op = Op()
        op.eng = eng
        op.fn = fn
        op.is_dma = dma is not None
        op.key = dma
        op.signal = False
        op.cnt = 0
        op.out = out
        return op

    def add(self, eng, fn, r=(), w=(), dma=None, out=False):
        op = self._mk(eng, fn, dma, out)
        op.desc = "%s r=%s w=%s" % (eng, list(r), list(w))
        psr = [x for x in r if x.startswith("ps")]
        if psr:
            r = [x for x in r if not x.startswith("ps")]
            w = list(w) + psr
        deps = []
        seen = set()

        def push(o):
            if o is not None and id(o) not in seen:
                seen.add(id(o))
                deps.append(o)
        for x in r:
            push(self.last_w.get(x))
        for x in w:
            push(self.last_w.get(x))
            for rd in self.readers.get(x, ()):
                push(rd)
        if op.is_dma and self.dma_chain.get(eng, 0) > 0:
            lst = self.dma_hist.setdefault(eng, [])
            k = self.dma_chain[eng]
            if len(lst) >= k:
                push(lst[-k])
            lst.append(op)
        if self.fence_op is not None and eng not in self.synced:
            push(self.fence_op)
            self.synced.add(eng)
        op.deps = deps
        for x in r:
            self.readers.setdefault(x, []).append(op)
        for x in w:
            self.last_w[x] = op
            self.readers[x] = []
        self.ops.append(op)
        self.last_op[eng] = op
        if op.is_dma:
            self.dmas_since.append(op)
        if out:
            self.out_dmas.append(op)
        return op

    def fence(self):
        ft = self.fence_tile
        op = self._mk("pool", lambda e: e.memset(ft, 0.0), None, False)
        op.desc = "FENCE"
        deps = [o for o in self.last_op.values()] + list(self.dmas_since)
        op.deps = deps
        self.ops.append(op)
        self.last_op["pool"] = op
        self.fence_op = op
        self.synced = {"pool"}
        self.dmas_since = []

    def _need(self, a, b):
        if a.is_dma or b.is_dma:
            return True
        if a.eng != b.eng:
            return True
        if a.eng == "pe":
            return False
        return SAME_ENGINE_SYNC

    def emit(self):
        nc = self.nc
        for b in self.ops:
            b.deps = [a for a in b.deps if self._need(a, b)]
            for a in b.deps:
                a.signal = True
        for o in self.ops:
            if o.is_dma:
                o.signal = True
        eng_sem, dma_sem, eng_cnt, dma_cnt = {}, {}, {}, {}
        for o in self.ops:
            if not o.signal:
                continue
            if o.is_dma:
                if o.key not in dma_sem:
                    dma_sem[o.key] = nc.alloc_semaphore(name="d%d" % len(dma_sem))
                    dma_cnt[o.key] = 0
                dma_cnt[o.key] += 16
                o.cnt = dma_cnt[o.key]
            else:
                if o.eng not in eng_sem:
                    eng_sem[o.eng] = nc.alloc_semaphore(name="e_" + o.eng)
                    eng_cnt[o.eng] = 0
                eng_cnt[o.eng] += 1
                o.cnt = eng_cnt[o.eng]
        self.n_sems = len(eng_sem) + len(dma_sem)
        waited = {e: {} for e in self.eng}
        acts = {e: [] for e in self.eng}
        n_wait = 0
        for b in self.ops:
            wl = waited[b.eng]
            need = {}
            for a in b.deps:
                sem = dma_sem[a.key] if a.is_dma else eng_sem[a.eng]
                k = id(sem)
                if a.cnt > wl.get(k, 0) and a.cnt > need.get(k, (None, 0))[1]:
                    need[k] = (sem, a.cnt)
            for k, (sem, val) in need.items():
                acts[b.eng].append(("w", sem, val))
                wl[k] = val
                n_wait += 1
            if b.signal:
                acts[b.eng].append(("i", b.fn, dma_sem[b.key] if b.is_dma else eng_sem[b.eng],
                                    16 if b.is_dma else 1, b.desc, b.cnt))
            else:
                acts[b.eng].append(("i", b.fn, None, 0, b.desc, 0))
        fin = {}
        for o in self.ops:
            if not o.is_dma:
                continue
            sem = dma_sem[o.key]
            if o.cnt > fin.get(id(sem), (None, 0))[1]:
                fin[id(sem)] = (sem, o.cnt)
        for k, (sem, val) in fin.items():
            acts["sp"].append(("w", sem, val))
        self.n_wait = n_wait
        self.n_ops = len(self.ops)
        self.acts = acts
        self.sem_names = {id(v): k for k, v in list(eng_sem.items()) + list(dma_sem.items())}

        def run(e, lst):
            for a in lst:
                if a[0] == "w":
                    e.wait_ge(a[1], a[2])
                else:
                    ins = a[1](e)
                    if a[2] is not None:
                        ins.then_inc(a[2], a[3])
        with nc.Block() as block:
            @block.sync
            def _(e):
                run(e, acts["sp"])

            @block.tensor
            def _(e):
                run(e, acts["pe"])

            @block.scalar
            def _(e):
                run(e, acts["act"])

            @block.vector
            def _(e):
                run(e, acts["dve"])

            @block.gpsimd
            def _(e):
                run(e, acts["pool"])


def f_dma(out, in_):
    return lambda e: e.dma_start(out=out, in_=in_)


def f_mm(out, lhsT, rhs, start, stop):
    return lambda e: e.matmul(out, lhsT=lhsT, rhs=rhs, start=start, stop=stop)


def f_tr(out, in_, ident):
    return lambda e: e.transpose(out, in_, ident)


def f_copy(out, in_):
    def fn(e):
        if hasattr(e, "tensor_copy"):
            return e.tensor_copy(out=out, in_=in_)
        return e.activation(out=out, in_=in_, func=AF.Copy)
    return fn


def f_act(out, in_, func, bias=None, scale=None, accum_out=None):
    kw = {}
    if bias is not None:
        kw["bias"] = bias
    if scale is not None:
        kw["scale"] = scale
    if accum_out is not None:
        kw["accum_out"] = accum_out
    return lambda e: e.activation(out=out, in_=in_, func=func, **kw)


def f_tt(out, in0, in1, op):
    return lambda e: e.tensor_tensor(out=out, in0=in0, in1=in1, op=op)


def f_ts(out, in0, s1, s2, op0, op1=None):
    if op1 is None:
        return lambda e: e.tensor_scalar(out=out, in0=in0, scalar1=s1, scalar2=None, op0=op0)
    return lambda e: e.tensor_scalar(out=out, in0=in0, scalar1=s1, scalar2=s2, op0=op0, op1=op1)


def f_stt(out, in0, scalar, in1, op0, op1):
    return lambda e: e.scalar_tensor_tensor(out=out, in0=in0, scalar=scalar, in1=in1, op0=op0, op1=op1)


def f_red(out, in_, op, axis=AX.X):
    return lambda e: e.tensor_reduce(out=out, in_=in_, axis=axis, op=op)


def f_memset(ap, v):
    return lambda e: e.memset(ap, v)


def f_recip(out, in_):
    return lambda e: e.reciprocal(out=out, in_=in_)


def f_bnstats(out, in_):
    return lambda e: e.bn_stats(out=out, in_=in_)


def f_bnaggr(out, in_):
    return lambda e: e.bn_aggr(out=out, in_=in_)


def build(enable):
    nc = bass.Bass("TRN2", target_bir_lowering=False)

    def din(name, shape):
        return nc.dram_tensor(name, list(shape), F32, kind="ExternalInput")

    def dout(name, shape):
        return nc.dram_tensor(name, list(shape), F32, kind="ExternalOutput")

    def dscr(name, shape, dt):
        return nc.dram_tensor(name, list(shape), dt, kind="Internal")

    xo = din("xo", [T, 1024]).ap()
    xh = din("xh", [NHALO * 128, 1024]).ap()
    xs = din("xs", [128, 1024]).ap()
    cache = din("cache", [4, 2048, 1024])
    state = din("state", [4, 4, 128, 128]).ap()
    w_in = din("w_in", [1024, 3584]).ap()
    relb = din("relb", [32, 32]).ap()
    w_out = din("w_out", [1024, 1024]).ap()
    ln1g = din("ln1g", [1, 1024])
    ln1b = din("ln1b", [1, 1024])
    w_up = din("w_up", [1024, 4096]).ap()
    w_down = din("w_down", [4096, 1024]).ap()
    ln2g = din("ln2g", [1, 1024])
    ln2b = din("ln2b", [1, 1024])
    c_ident = din("c_ident", [128, 128]).ap()
    c_jflip = din("c_jflip", [128, 128]).ap()
    c_cs = din("c_cs", [(NHALO + 17) * 128, 128]).ap()
    c_vtab = din("c_vtab", [128, 33]).ap()
    c_dmt = din("c_dmt", [128, 512]).ap()
    c_qdec = din("c_qdec", [128, 512]).ap()
    c_kdec = din("c_kdec", [128, 512]).ap()
    c_oh = din("c_oh", [32, 1152 + 387]).ap()
    c_fmask = din("c_fmask", [8, 1152]).ap()
    c_gam = din("c_gam", [128, 512]).ap()
    c_delta = din("c_delta", [128, 4]).ap()
    c_drow = din("c_drow", [128, 16]).ap()

    y_p = dout("y_p", [T, 1024]).ap()
    y_s = dout("y_s", [4, 1024]).ap()
    kvp = dout("kvp", [T, 1024]).ap()
    kvs = dout("kvs", [4, 1024]).ap()
    srp = dout("srp", [4, 128, 128]).ap()
    srs = dout("srs", [4, 4, 128, 128]).ap()

    QKV = dscr("QKV", [4096 + 128, ROW], BF16)
    OL = dscr("OL", [T, 512], F32).ap()
    GS = dscr("GS", [T + 128, 512], BF16).ap()
    ATT = dscr("ATT", [64, 8 * T], BF16).ap()
    FSC = dscr("FSC", [8, 1152], F32)
    SQ = dscr("SQ", [3, 4, 512], F32).ap()
    SATT = dscr("SATT", [32, 64], F32).ap()

    def qkv_rows(base, step, c0, c1):
        return bass.AP(QKV, base * ROW + c0, [[step * ROW, 128], [1, c1 - c0]])

    def sb(name, cols, dt, parts=128):
        return nc.alloc_sbuf_tensor(name, [parts, cols], dt).ap()

    ps = [nc.alloc_psum_tensor("ps%d" % i, [128, 512], F32).ap() for i in range(8)]
    psb = [p.bitcast(BF16) for p in ps]

    fence_tile = sb("fence_t", 8, F32)
    P = Prog(nc, fence_tile)

    identb = sb("identb", 128, BF16)
    identf = sb("identf", 128, F32)
    jflip = sb("jflipf", 128, F32)
    onesf = sb("onesf", 128, F32)
    e64 = sb("e64", 64, F32)
    QdecT = sb("QdecT", 4 * 2176, BF16).rearrange("p (h t) -> p h t", h=4)
    w_out_sb = sb("w_out_sb", 8 * 1024, BF16).rearrange("p (c n) -> p c n", c=8)
    lng = sb("lng", 1024, F32)
    lnb = sb("lnb", 1024, F32)
    S_f = sb("S_f", 512, F32)
    S_b = sb("S_b", 512, BF16)
    negm = sb("negm", 1, F32)
    vtab = sb("vtab", 33, F32)
    gam = sb("gam", 512, F32)
    delta = sb("delta", 4, F32)
    drow = sb("drow", 16, F32)
    stats = sb("stats", 64, F32)
    att_s = sb("att_s", 512, BF16).rearrange("p (c t) -> p c t", c=4)
    oret_s = sb("oret_s", 512, F32)
    X = sb("arenaX", 32768, BF16)
    Y = sb("arenaY", 34816, BF16)

    class Arena:
        def __init__(self, ap, size):
            self.ap = ap
            self.size = size
            self.off = 0

        def reset(self):
            self.off = 0

        def get(self, cols, dt):
            nb = cols * (2 if dt == BF16 else 4)
            nb = (nb + 31) // 32 * 32
            assert self.off + nb <= self.size, (self.off, nb, self.size)
            a = self.ap[:, self.off // 2:(self.off + nb) // 2]
            self.off += nb
            if dt == F32:
                return a.bitcast(F32)[:, 0:cols]
            return a[:, 0:cols]

    AX_ = Arena(X, 57344)
    AY_ = Arena(Y, 69632)

    P.add("pool", f_dma(identb, c_ident), w=["identb"], dma="c0")
    P.add("sp", f_dma(identf, c_ident), w=["identf"], dma="k1_1")
    P.add("sp", f_dma(jflip, c_jflip), w=["jflip"], dma="k1_2")
    P.add("sp", f_dma(vtab, c_vtab), w=["vtab"], dma="k1_4")
    P.add("sp", f_dma(gam, c_gam), w=["gam"], dma="k1_5")
    P.add("sp", f_dma(delta, c_delta), w=["delta"], dma="k1_6")
    P.add("sp", f_dma(drow, c_drow), w=["drow"], dma="k1_7")
    P.add("dve", f_memset(onesf, 1.0), w=["onesf"])
    P.add("dve", f_memset(e64, 0.0), w=["e64"])
    P.add("dve", f_memset(e64[64:65, :], 1.0), r=["e64"], w=["e64"])
    P.add("dve", f_memset(negm, 0.0), w=["negm"])
    P.add("dve", f_memset(S_f, 0.0), w=["S_f"])
    P.add("dve", f_memset(S_b, 0.0), w=["S_b"])
    P.add("dve", f_memset(oret_s, 0.0), w=["oret_s"])
    P.add("dve", f_memset(att_s, 0.0), w=["att_s"])

    AX_.reset()
    AY_.reset()
    w_in_sb = AX_.get(8 * 3584, BF16).rearrange("p (c n) -> p c n", c=8)
    for c in range(8):
        P.add("pool", f_dma(w_in_sb[:, c, :], w_in[c * 128:(c + 1) * 128, :]), w=["w_in%d" % c], dma="w_in%d" % c)
    cs_t = [AY_.get(128, F32) for _ in range(3)]
    dmt = AY_.get(512, F32)
    qdec = AY_.get(512, F32)
    kdec = AY_.get(512, F32)
    P.add("sp", f_dma(dmt, c_dmt), w=["dmt"], dma="k1_9")
    P.add("sp", f_dma(qdec, c_qdec), w=["qdec"], dma="k1_10")
    P.add("sp", f_dma(kdec, c_kdec), w=["kdec"], dma="k1_11")
    xb = [AY_.get(1024, BF16) for _ in range(2)]
    xT = [AY_.get(1024, BF16) for _ in range(2)]
    qkv_st = [AY_.get(ROW, BF16) for _ in range(2)]
    kv_f = [AY_.get(1024, F32) for _ in range(2)]
    rt = [AY_.get(256, F32) for _ in range(4)]
    qrot_bb = [AY_.get(512, BF16) for _ in range(3)]
    krot_bb = [AY_.get(512, BF16) for _ in range(3)]
    kdec_b = AY_.get(512, BF16)
    krT = AY_.get(512, BF16)
    qrT = AY_.get(512, BF16)
    v_bb = [AY_.get(512, BF16) for _ in range(3)]
    g_st = [AY_.get(512, BF16) for _ in range(2)]
    sT_b = AY_.get(512, BF16)
    sqt = AY_.get(512, F32)
    redq = AY_.get(8, F32)
    mxq = AY_.get(8, F32)
    mxk = AY_.get(8, F32)
    mfin = AY_.get(8, F32)
    P.add("pool", f_memset(mxq, 0.0), w=["mxq"])
    P.add("pool", f_memset(mxk, 0.0), w=["mxk"])
    o_st = [AY_.get(512, F32) for _ in range(2)]
    xtail = X[:, 28672:32768].bitcast(F32)
    qrot_f = xtail[:, 0:512]
    krot_f = xtail[:, 512:1024]
    v_f = xtail[:, 1024:1536]
    sq_f = xtail[:, 1536:2048]

    proj_banks = [1, 2, 3, 4]
    st_ = {"pc": 0, "tc": 0, "rc": 0}

    def nbank(kind):
        if kind == "proj":
            b = proj_banks[st_["pc"] % 4]
            st_["pc"] += 1
        else:
            b = (6, 7)[st_["rc"] % 2]
            st_["rc"] += 1
        return b

    def prefetchA(kind, idx, n):
        s = n % 2
        src = {"halo": xh, "own": xo, "samp": xs}[kind]
        rows = src[idx * 128:(idx + 1) * 128, :] if kind != "samp" else src
        P.add("pool", f_dma(xb[s], rows), w=["xb%d" % s], dma="xb%d" % s)
        csrow = {"halo": idx, "own": NHALO + idx, "samp": NHALO + 16}[kind]
        s3c = n % 3
        P.add("sp", f_dma(cs_t[s3c], c_cs[csrow * 128:(csrow + 1) * 128, :]), w=["cs%d" % s3c], dma="cs%d" % s3c)

    def phaseA_tile(kind, idx, n):
        s = n % 2
        s3 = n % 3
        qrot_b = qrot_bb[s3]
        krot_b = krot_bb[s3]
        v_b = v_bb[s3]
        sfx = "" if kind == "samp" else str(s3)
        for c in range(8):
            P.add("pe", f_tr(psb[0][:, c * 128:(c + 1) * 128], xb[s][:, c * 128:(c + 1) * 128], identb),
                  r=["xb%d" % s, "identb"], w=["ps0"])
        P.add("act", f_copy(xT[s], psb[0]), r=["ps0"], w=["xTa%d" % s, "xTb%d" % s])
        def front_b():
            jb = NHALO - idx if kind == "halo" else 0
            if kind == "halo":
                chunks = ([1, 2] if jb <= 16 else []) + [4, 5]
                h0 = [h for h in range(4) if jb <= HJ[h]][0]
            else:
                chunks = [0, 1, 2, 3, 4, 5, 6]
                h0 = 0
            nh = 4 - h0
            if "chunks" in enable:
                chunks = enable["chunks"]
            tcol = {"halo": idx - (NHALO - 16), "own": 16 + idx, "samp": 32}[kind]
            row0 = {"halo": (idx - (NHALO - 16)) * 128, "own": 2048 + idx * 128, "samp": 4096}[kind]
            for j in chunks:
                b = nbank("proj")
                pb = "ps%d" % b
                c0_ = h0 * 128 if j in (4, 5) else 0
                for c in range(8):
                    P.add("pe", f_mm(ps[b][:, c0_:512], xT[s][:, c * 128:(c + 1) * 128],
                                     w_in_sb[:, c, j * 512 + c0_:(j + 1) * 512], c == 0, c == 7),
                          r=["xTa%d" % s, "xTb%d" % s, "w_in%d" % c], w=[pb])
                if j == 0:
                    P.add("act", f_act(qkv_st[s][:, 0:512], ps[b], AF.Copy, scale=0.125), r=[pb], w=["qst_q%d" % s])
                    if kind == "own":
                        P.add("dve", f_tt(sqt, qkv_st[s][:, 0:512], qkv_st[s][:, 0:512], ALU.mult), r=["qst_q%d" % s],
                              w=["sqt"])
                        P.add("dve", f_red(redq, sqt.rearrange("p (h d) -> p h d", h=8), ALU.add), r=["sqt"], w=["redq"])
                        P.add("dve", f_tt(mxq, mxq, redq, ALU.max), r=["redq", "mxq"], w=["mxq"])
                    if kind == "samp":
                        P.add("act", f_act(sq_f, ps[b], AF.Copy, scale=0.125), r=[pb], w=["sq_f"])
                        P.add("sp", f_dma(SQ[0], sq_f[0:4, :]), r=["sq_f"], w=["SQ0"], dma="sq0")
                elif j == 1:
                    if kind != "halo":
                        P.add("dve", f_copy(kv_f[s][:, 0:512], ps[b]), r=[pb], w=["kvf_k%d" % s])
                    P.add("act", f_copy(qkv_st[s][:, 512:1024], ps[b]), r=[pb], w=["qst_k%d" % s])
                    if kind != "samp":
                        P.add("dve", f_tt(sqt, qkv_st[s][:, 512:1024], qkv_st[s][:, 512:1024], ALU.mult),
                              r=["qst_k%d" % s], w=["sqt"])
                        P.add("dve", f_red(redq, sqt.rearrange("p (h d) -> p h d", h=8), ALU.add), r=["sqt"], w=["redq"])
                        P.add("dve", f_tt(mxk, mxk, redq, ALU.max), r=["redq", "mxk"], w=["mxk"])
                elif j == 2:
                    if kind != "halo":
                        P.add("dve", f_copy(kv_f[s][:, 512:1024], ps[b]), r=[pb], w=["kvf_v%d" % s])
                    va = qkv_st[s][:, 1024:1544].rearrange("p (h d) -> p h d", h=8)
                    P.add("act", f_copy(va[:, :, 0:64], ps[b].rearrange("p (h d) -> p h d", h=8)), r=[pb],
                          w=["qst_v%d" % s])
                    P.add("pool", f_copy(va[:, :, 64:65], vtab[:, tcol:tcol + 1].unsqueeze(1).broadcast_to([128, 8, 1])),
                          r=["vtab"], w=["qst_o%d" % s])
                    lo = 512 if kind == "halo" else 0
                    rr = ["qst_k%d" % s, "qst_v%d" % s, "qst_o%d" % s] + ([] if kind == "halo" else ["qst_q%d" % s])
                    P.add("sp", f_dma(qkv_rows(row0, 1, lo, ROW), qkv_st[s][:, lo:ROW]), r=rr,
                          w=["QKV_%s%d" % (kind, idx)], dma="qkvw%d" % s)
                    if kind == "own":
                        P.add("sp", f_dma(kvp[idx * 128:(idx + 1) * 128, :], kv_f[s]), r=["kvf_k%d" % s, "kvf_v%d" % s],
                              dma="kvpo%d" % s, out=True)
                    elif kind == "samp":
                        P.add("sp", f_dma(kvs, kv_f[s][0:4, :]), r=["kvf_k%d" % s, "kvf_v%d" % s],
                              dma="kvso", out=True)
                        P.add("sp", f_dma(SQ[1], kv_f[s][0:4, 0:512]), r=["kvf_k%d" % s], w=["SQ1"], dma="sq1")
                        P.add("sp", f_dma(SQ[2], kv_f[s][0:4, 512:1024]), r=["kvf_v%d" % s], w=["SQ2"], dma="sq2")
                elif j in (3, 4):
                    psv = ps[b].rearrange("p (h two d) -> p h two d", h=4, two=2)
                    x1 = psv[:, h0:4, 0, :]
                    x2 = psv[:, h0:4, 1, :]
                    cosb = cs_t[s3][:, 0:64].unsqueeze(1).broadcast_to([128, nh, 64])
                    sinb = cs_t[s3][:, 64:128].unsqueeze(1).broadcast_to([128, nh, 64])
                    rv = [t_.rearrange("p (h d) -> p h d", h=4)[:, h0:4, :] for t_ in rt]
                    cn = "cs%d" % s3
                    P.add("dve", f_tt(rv[0], x1, cosb, ALU.mult), r=[pb, cn], w=["rt0"])
                    P.add("dve", f_tt(rv[1], x2, sinb, ALU.mult), r=[pb, cn], w=["rt1"])
                    P.add("dve", f_tt(rv[2], x1, sinb, ALU.mult), r=[pb, cn], w=["rt2"])
                    P.add("dve", f_tt(rv[3], x2, cosb, ALU.mult), r=[pb, cn], w=["rt3"])
                    if kind == "samp":
                        dst = qrot_f if j == 3 else krot_f
                    else:
                        dst = qrot_b if j == 3 else krot_b
                    dname = ("qrot" if j == 3 else "krot") + sfx
                    dv = dst.rearrange("p (h two d) -> p h two d", h=4, two=2)
                    P.add("pool", f_tt(dv[:, h0:4, 0, :], rv[0], rv[1], ALU.subtract), r=["rt0", "rt1"], w=[dname + "a"])
                    P.add("pool", f_tt(dv[:, h0:4, 1, :], rv[2], rv[3], ALU.add), r=["rt2", "rt3"], w=[dname + "b"])
                elif j == 5:
                    if kind == "samp":
                        P.add("act", f_copy(v_f, ps[b]), r=[pb], w=["v_f"])
                    else:
                        P.add("act", f_copy(v_b[:, h0 * 128:512], ps[b][:, h0 * 128:512]), r=[pb], w=["v_b" + sfx])
                elif j == 6:
                    gs = n % 2
                    P.add("act", f_act(g_st[gs], ps[b], AF.Silu), r=[pb], w=["g_st%d" % gs])
                    grow = idx * 128 if kind == "own" else T
                    P.add("sp", f_dma(GS[grow:grow + 128, :], g_st[gs]), r=["g_st%d" % gs], w=["GS%d" % (grow // 128)],
                          dma="gsw%d" % gs)
            if kind == "samp" or "chunks" in enable:
                return None

            def back():
                hsl_ = slice(h0 * 128, 512)
                P.add("pool", f_tt(kdec_b[:, hsl_], krot_b[:, hsl_], kdec[:, hsl_], ALU.mult), r=["krota" + sfx, "krotb" + sfx, "kdec"],
                      w=["kdec_b"])
                if kind == "halo":
                    b3 = nbank("ret")
                    for h in range(h0, 4):
                        hs = slice(h * 128, (h + 1) * 128)
                        P.add("pe", f_mm(ps[b3][:, hs], kdec_b[:, hs], v_b[:, hs], True, True), r=["kdec_b", "v_b" + sfx],
                              w=["ps%d" % b3])
                    for h in range(h0, 4):
                        hs = slice(h * 128, (h + 1) * 128)
                        P.add("dve", f_stt(S_f[:, hs], S_f[:, hs], float(GAM[h] ** 128), ps[b3][:, hs], ALU.mult, ALU.add),
                              r=["ps%d" % b3, "S_f"], w=["S_f"])
                    if idx == NHALO - 1:
                        P.add("act", f_copy(S_b, S_f), r=["S_f"], w=["S_b"])
                    return
                i = idx
                for h in range(4):
                    P.add("pe", f_tr(psb[5][:, h * 128:(h + 1) * 128], qrot_b[:, h * 128:(h + 1) * 128], identb),
                          r=["qrota" + sfx, "qrotb" + sfx, "identb"], w=["ps5"])
                for h in range(4):
                    P.add("pe", f_tr(psb[5][:, 512 + h * 128:512 + (h + 1) * 128], krot_b[:, h * 128:(h + 1) * 128], identb),
                          r=["krota" + sfx, "krotb" + sfx, "identb"], w=["ps5"])
                P.add("act", f_copy(krT, psb[5][:, 512:1024]), r=["ps5"], w=["krT"])
                P.add("dve", f_copy(qrT, psb[5][:, 0:512]), r=["ps5"], w=["qrT"])
                P.add("dve", f_tt(QdecT[:, :, i * 128:(i + 1) * 128], psb[5][:, 0:512].rearrange("p (h t) -> p h t", h=4),
                                  qdec.rearrange("p (h t) -> p h t", h=4), ALU.mult),
                      r=["ps5", "qdec"], w=["QdecT%d" % i])
                b1 = nbank("ret")
                for h in range(4):
                    hs = slice(h * 128, (h + 1) * 128)
                    P.add("pe", f_mm(ps[b1][:, hs], krT[:, hs], qrT[:, hs], True, True), r=["krT", "qrT"], w=["ps%d" % b1])
                P.add("dve", f_tt(sT_b, ps[b1], dmt, ALU.mult), r=["ps%d" % b1, "dmt"], w=["sT_b"])
                b2 = nbank("ret")
                for h in range(4):
                    hs = slice(h * 128, (h + 1) * 128)
                    P.add("pe", f_mm(ps[b2][:, hs], sT_b[:, hs], v_b[:, hs], True, False), r=["sT_b", "v_b" + sfx], w=["ps%d" % b2])
                    P.add("pe", f_mm(ps[b2][:, hs], QdecT[:, h, i * 128:(i + 1) * 128], S_b[:, hs], False, True),
                          r=["QdecT%d" % i, "S_b"], w=["ps%d" % b2])
                os_ = n % 2
                P.add("act", f_copy(o_st[os_], ps[b2]), r=["ps%d" % b2], w=["o_st%d" % os_])
                P.add("sp", f_dma(OL[i * 128:(i + 1) * 128, :], o_st[os_]), r=["o_st%d" % os_], w=["OL%d" % i],
                      dma="olw%d" % os_)
                b3 = nbank("ret")
                for h in range(4):
                    hs = slice(h * 128, (h + 1) * 128)
                    P.add("pe", f_mm(ps[b3][:, hs], kdec_b[:, hs], v_b[:, hs], True, True), r=["kdec_b", "v_b" + sfx],
                          w=["ps%d" % b3])
                for h in range(4):
                    hs = slice(h * 128, (h + 1) * 128)
                    P.add("dve", f_stt(S_f[:, hs], S_f[:, hs], float(GAM[h] ** 128), ps[b3][:, hs], ALU.mult, ALU.add),
                          r=["ps%d" % b3, "S_f"], w=["S_f"])
                P.add("act", f_copy(S_b, S_f), r=["S_f"], w=["S_b"])
            return back
        return front_b

    n = 0
    pend_q = []
    tiles_ = [("halo", i) for i in range(enable.get("halo_start", 0), enable.get("nhalo", NHALO))]
    tiles_ += [("own", i) for i in range(enable.get("nown", 16))]
    if enable.get("samp_tile", True):
        tiles_ += [("samp", 0)]
    NT_ = len(tiles_)
    for k_ in range(min(2, NT_)):
        prefetchA(tiles_[k_][0], tiles_[k_][1], k_)
    fb_q = []
    if NT_:
        fb_q.append(phaseA_tile(tiles_[0][0], tiles_[0][1], 0))
    for ti_ in range(NT_):
        if ti_ + 2 < NT_:
            prefetchA(tiles_[ti_ + 2][0], tiles_[ti_ + 2][1], ti_ + 2)
        if ti_ + 1 < NT_:
            fb_q.append(phaseA_tile(tiles_[ti_ + 1][0], tiles_[ti_ + 1][1], ti_ + 1))
        bk_ = fb_q.pop(0)()
        pend_q.append(bk_)
        if len(pend_q) > 2:
            f_ = pend_q.pop(0)
            if f_ is not None:
                f_()
    for f_ in pend_q:
        if f_ is not None:
            f_()
    if enable.get("stop_at") == "tiles":
        P.emit()
        return nc, P

    P.add("dve", f_red(mfin[:, 0:1], mxq, ALU.max), r=["mxq"], w=["mfin0"])
    P.add("dve", f_red(mfin[:, 1:2], mxk, ALU.max), r=["mxk"], w=["mfin1"])
    P.add("pe", f_tr(ps[6][0:2, 0:128], mfin[:, 0:2], identf), r=["mfin0", "mfin1", "identf"], w=["ps6"])
    P.add("dve", f_red(mfin[0:2, 2:3], ps[6][0:2, 0:128], ALU.max), r=["ps6"], w=["mfin2"])
    P.add("pe", f_tr(ps[7][0:1, 0:2], mfin[0:2, 2:3], identf[0:2, 0:2]), r=["mfin2", "identf"], w=["ps7"])
    P.add("dve", f_copy(mfin[0:1, 5:7], ps[7][0:1, 0:2]), r=["ps7"], w=["mfin5"])
    P.add("dve", f_tt(mfin[0:1, 3:4], mfin[0:1, 5:6], mfin[0:1, 6:7], ALU.mult), r=["mfin5"], w=["mfin3"])
    P.add("act", f_act(mfin[0:1, 3:4], mfin[0:1, 3:4], AF.Sqrt), r=["mfin3"], w=["mfin3"])
    P.add("dve", f_ts(mfin[0:1, 4:5], mfin[0:1, 3:4], -1.001, None, ALU.mult), r=["mfin3"], w=["mfin4"])
    P.add("pe", f_mm(ps[6][:, 0:1], onesf[0:1, :], mfin[0:1, 4:5], True, True), r=["mfin4", "onesf"], w=["ps6"])
    P.add("dve", f_copy(negm, ps[6][:, 0:1]), r=["ps6"], w=["negm"])
    P.add("sp", f_dma(srp.rearrange("h k v -> k h v"), S_f.rearrange("p (h v) -> p h v", h=4)), r=["S_f"],
          dma="srpo", out=True)
    P.fence()

    AX_.reset()
    AY_.reset()
    EB = AY_.get(3 * 8 * 256, BF16).rearrange("p (i h q) -> p i h q", i=3, h=8)
    relb_sb = AX_.get(32, F32)
    oh_sb = AX_.get(1152 + 387, F32)
    fm_sb = AX_.get(1152, F32)
    F_sb = AX_.get(1152, F32)
    bs_sb = AY_.get(387, F32)
    hank = [AX_.get(256, F32) for _ in range(6)]
    P.add("sp", f_dma(relb_sb[0:32, :], relb), w=["relb"], dma="k2_1")
    P.add("sp", f_dma(oh_sb[0:32, :], c_oh), w=["oh"], dma="k2_2")
    P.add("sp", f_dma(fm_sb[0:8, :], c_fmask), w=["fm"], dma="k2_3")
    for i in range(3):
        P.add("pe", f_mm(ps[1 + i][0:32, 0:384], relb_sb[0:32, 0:32], oh_sb[0:32, i * 384:(i + 1) * 384], True, True),
              r=["relb", "oh"], w=["ps%d" % (1 + i)])
        P.add("act", f_act(F_sb[0:8, i * 384:(i + 1) * 384], ps[1 + i][0:8, 0:384], AF.Exp), r=["ps%d" % (1 + i)],
              w=["F_sb%d" % i])
    P.add("pe", f_mm(ps[4][0:32, 0:387], relb_sb[0:32, 0:32], oh_sb[0:32, 1152:1152 + 387], True, True),
          r=["relb", "oh"], w=["ps4"])
    P.add("dve", f_copy(bs_sb[0:32, :], ps[4][0:32, 0:387]), r=["ps4"], w=["bs"])
    P.add("dve", f_tt(F_sb[0:8, :], F_sb[0:8, :], fm_sb[0:8, :], ALU.mult), r=["F_sb0", "F_sb1", "F_sb2", "fm"],
          w=["F_sb"])
    P.add("sp", f_dma(FSC.ap(), F_sb[0:8, :]), r=["F_sb"], w=["FSC"], dma="fsc")
    k_ = 0
    for i in range(3):
        for h in range(8):
            hs = k_ % 6
            pbk = (5, 6, 7)[k_ % 3]
            P.add("sp", f_dma(hank[hs], bass.AP(FSC, h * 1152 + i * 384, [[1, 128], [1, 256]])), r=["FSC"],
                  w=["hank%d" % hs], dma="hank%d" % hs)
            P.add("pe", f_mm(ps[pbk][:, 0:256], jflip, hank[hs], True, True), r=["jflip", "hank%d" % hs],
                  w=["ps%d" % pbk])
            P.add("dve" if k_ % 2 else "act", f_copy(EB[:, i, h, :], ps[pbk][:, 0:256]), r=["ps%d" % pbk],
                  w=["EB"])
            k_ += 1


    if enable.get("samp", True):
        P.fence()
        AX_.reset()
        tapbuf = AX_.get(128 * 64, BF16).rearrange("p (j d) -> p j d", j=128)
        prod = AX_.get(32 * 64, F32).rearrange("p (j d) -> p j d", j=32)
        Lg = AX_.get(387, F32)
        Pp = AX_.get(387, F32)
        q32 = AX_.get(64, F32)
        k32 = AX_.get(64, F32)
        v32 = AX_.get(64, F32)
        opart = AX_.get(13 * 64, F32).rearrange("p (k d) -> p k d", k=13)
        o32 = AX_.get(64, F32)
        o16 = AX_.get(128, F32)
        sm = AX_.get(8, F32)
        st = AX_.get(16 * 128, F32).rearrange("p (g v) -> p g v", g=16)
        Vm = AX_.get(4 * 512, F32).rearrange("p (b n) -> p b n", b=4)
        qTs = AX_.get(16, F32).rearrange("p (h m) -> p h m", h=4)
        qTm = AX_.get(64, F32).rearrange("p (h b m) -> p h b m", h=4, b=4)
        prod2 = AX_.get(512, F32)
        oin = AX_.get(512, F32)
        qk = AX_.get(4, F32)
        S32 = slice(0, 32)
        P.add("sp", f_dma(q32[S32, :], SQ[0].rearrange("b (h d) -> (b h) d", h=8)), r=["SQ0"], w=["q32"], dma="q32")
        P.add("sp", f_dma(k32[S32, :], SQ[1].rearrange("b (h d) -> (b h) d", h=8)), r=["SQ1"], w=["k32"], dma="k32")
        P.add("sp", f_dma(v32[S32, :], SQ[2].rearrange("b (h d) -> (b h) d", h=8)), r=["SQ2"], w=["v32"], dma="v32")

        def load_taps(dil, voff):
            for b in range(4):
                src = bass.AP(cache, b * 2048 * 1024 + (2048 - 128 * dil) * 1024 + voff,
                              [[64, 8], [dil * 1024, 128], [1, 64]])
                P.add("pool", f_dma(tapbuf[b * 8:(b + 1) * 8, :, :], src), w=["tapbuf%d" % b], dma="taps%d" % b)
        for i, dil in enumerate(DILS):
            load_taps(dil, 0)
            for cq in range(4):
                P.add("dve", f_tt(prod[S32], tapbuf[S32, cq * 32:(cq + 1) * 32, :],
                                  q32[S32, :].unsqueeze(1).broadcast_to([32, 32, 64]), ALU.mult),
                      r=["tapbuf0", "tapbuf1", "tapbuf2", "tapbuf3", "q32"], w=["prod"])
                P.add("dve", f_red(Lg[S32, i * 129 + cq * 32:i * 129 + (cq + 1) * 32], prod[S32], ALU.add),
                      r=["prod"], w=["Lg"])
        P.add("dve", f_tt(prod[S32, 0, :], k32[S32, :], q32[S32, :], ALU.mult), r=["k32", "q32"], w=["prod"])
        P.add("dve", f_red(sm[S32, 0:1], prod[S32, 0, :], ALU.add), r=["prod"], w=["sm0"])
        for i in range(3):
            P.add("dve", f_copy(Lg[S32, i * 129 + 128:i * 129 + 129], sm[S32, 0:1]), r=["sm0"], w=["Lg"])
        P.add("dve", f_tt(Lg[S32], Lg[S32], bs_sb[S32], ALU.add), r=["Lg", "bs"], w=["Lg"])
        P.add("dve", f_red(sm[S32, 1:2], Lg[S32], ALU.max), r=["Lg"], w=["sm1"])
        P.add("dve", f_ts(sm[S32, 2:3], sm[S32, 1:2], -1.0, None, ALU.mult), r=["sm1"], w=["sm2"])
        P.add("dve", f_memset(sm[S32, 3:4], 0.0), w=["sm3"])
        P.add("act", f_act(Pp[S32], Lg[S32], AF.Exp, bias=sm[S32, 2:3], scale=1.0, accum_out=sm[S32, 3:4]),
              r=["Lg", "sm2", "sm3"], w=["Pp", "sm3"])
        for i, dil in enumerate(DILS):
            load_taps(dil, 512)
            for cq in range(4):
                P.add("dve", f_tt(prod[S32], tapbuf[S32, cq * 32:(cq + 1) * 32, :],
                                  Pp[S32, i * 129 + cq * 32:i * 129 + (cq + 1) * 32].unsqueeze(2).broadcast_to([32, 32, 64]),
                                  ALU.mult), r=["tapbuf0", "tapbuf1", "tapbuf2", "tapbuf3", "Pp"], w=["prod"])
                P.add("dve", f_red(opart[S32, i * 4 + cq, :], prod[S32].rearrange("p j d -> p d j"), ALU.add),
                      r=["prod"], w=["opart"])
        P.add("dve", f_tt(sm[S32, 4:5], Pp[S32, 128:129], Pp[S32, 257:258], ALU.add), r=["Pp"], w=["sm4"])
        P.add("dve", f_tt(sm[S32, 4:5], sm[S32, 4:5], Pp[S32, 386:387], ALU.add), r=["Pp", "sm4"], w=["sm4"])
        P.add("dve", f_ts(opart[S32, 12, :], v32[S32, :], sm[S32, 4:5], None, ALU.mult), r=["v32", "sm4"],
              w=["opart"])
        P.add("dve", f_red(o32[S32], opart[S32].rearrange("p k d -> p d k"), ALU.add), r=["opart"], w=["o32"])
        P.add("dve", f_recip(sm[S32, 5:6], sm[S32, 3:4]), r=["sm3"], w=["sm5"])
        P.add("dve", f_ts(o32[S32], o32[S32], sm[S32, 5:6], None, ALU.mult), r=["o32", "sm5"], w=["o32"])
        P.add("sp", f_dma(SATT, o32[S32, :]), r=["o32"], w=["SATT"], dma="satt")
        P.add("sp", f_dma(o16[0:16, :], SATT.rearrange("(g par) d -> g (par d)", par=2)), r=["SATT"], w=["o16"],
              dma="o16")
        P.add("pe", f_tr(ps[1][:, 0:16], o16[0:16, :], identf[0:16, 0:16]), r=["o16", "identf"], w=["ps1"])
        P.add("act", f_copy(att_s[:, 0:4, 0:4], ps[1][:, 0:16].rearrange("p (b q) -> p q b", b=4)), r=["ps1", "att_s"],
              w=["att_s"])
        P.add("pool", f_ts(krot_f, krot_f, float(128.0 ** -0.5), None, ALU.mult), r=["krota", "krotb"],
              w=["krota", "krotb"])
        P.add("sp", f_dma(st, state.rearrange("b h k v -> k (b h) v")), w=["st"], dma="st")
        for b in range(4):
            P.add("dve", f_ts(Vm[0:4, b, :], v_f[0:4, :], delta[0:4, b:b + 1], None, ALU.mult), r=["v_f", "delta"],
                  w=["Vm"])
        for h in range(4):
            P.add("pe", f_tr(ps[2][:, h * 128:(h + 1) * 128], qrot_f[:, h * 128:(h + 1) * 128], identf),
                  r=["qrota", "qrotb", "identf"], w=["ps2"])
        P.add("act", f_copy(qTs, ps[2].rearrange("p (h t) -> p h t", h=4)[:, :, 0:4]), r=["ps2"], w=["qTs"])
        for b in range(4):
            P.add("dve", f_tt(qTm[:, :, b, :], qTs, drow[:, b * 4:(b + 1) * 4].unsqueeze(1).broadcast_to([128, 4, 4]),
                              ALU.mult), r=["qTs", "drow"], w=["qTm"])
        for h in range(4):
            for b in range(4):
                P.add("pe", f_mm(ps[3][0:4, h * 128:(h + 1) * 128], qTm[:, h, b, :], st[:, b * 4 + h, :], b == 0,
                                 b == 3), r=["qTm", "st"], w=["ps3"])
        P.add("dve", f_tt(prod2[0:4, :], qrot_f[0:4, :], krot_f[0:4, :], ALU.mult), r=["qrota", "qrotb", "krota", "krotb"],
              w=["prod2"])
        P.add("dve", f_red(qk[0:4, :], prod2[0:4, :].rearrange("p (h d) -> p h d", h=4), ALU.add), r=["prod2"],
              w=["qk"])
        P.add("dve", f_tt(oin[0:4, :].rearrange("p (h d) -> p h d", h=4),
                          v_f[0:4, :].rearrange("p (h d) -> p h d", h=4),
                          qk[0:4, :].unsqueeze(2).broadcast_to([4, 4, 128]), ALU.mult), r=["v_f", "qk"], w=["oin"])
        P.add("dve", f_tt(oret_s[0:4, :], ps[3][0:4, :], gam[0:4, :], ALU.mult), r=["ps3", "gam", "oret_s"],
              w=["oret_s"])
        P.add("dve", f_tt(oret_s[0:4, :], oret_s[0:4, :], oin[0:4, :], ALU.add), r=["oret_s", "oin"], w=["oret_s"])
        for b in range(4):
            bk = 4 + b
            for h in range(4):
                hs = slice(h * 128, (h + 1) * 128)
                P.add("pe", f_mm(ps[bk][:, hs], krot_f[0:4, hs], Vm[0:4, b, hs], True, True),
                      r=["krota", "krotb", "Vm"], w=["ps%d" % bk])
            for h in range(4):
                hs = slice(h * 128, (h + 1) * 128)
                P.add("dve", f_stt(st[:, b * 4 + h, :], st[:, b * 4 + h, :], float(GAM[h]), ps[bk][:, hs], ALU.mult,
                                   ALU.add), r=["ps%d" % bk, "st"], w=["st"])
        P.add("sp", f_dma(srs.rearrange("b h k v -> k (b h) v"), st), r=["st"], dma="srso", out=True)

    P.fence()
    if enable.get("stop_after_A", False):
        P.emit()
        return nc, P

    AX_.reset()
    AX_.size = 65536
    Oacc_full = AX_.get(8 * 2048, F32)
    Oacc = Oacc_full.rearrange("p (h t) -> p h t", h=8)
    kvt = [AY_.get(1032, BF16) for _ in range(8)]
    qt = [AY_.get(512, BF16) for _ in range(2)]
    KT = [AY_.get(512, BF16) for _ in range(8)]
    QTe = [AY_.get(512, BF16) for _ in range(2)]
    QTo = [AY_.get(512, BF16) for _ in range(2)]
    for q_ in range(2):
        P.add("pool", f_memset(QTe[q_], 0.0), w=["QT%d" % q_])
        P.add("pool", f_memset(QTo[q_], 0.0), w=["QT%d" % q_])
    P.add("pool", f_memset(Oacc_full, 0.0), w=["Oacc0", "Oacc1"])
    Pexp = [AY_.get(2048, BF16) for _ in range(2)]
    PTm = [AY_.get(2048, BF16) for _ in range(2)]
    cnt = {"kv": 0, "q": 0, "blk": 0, "tr": 0}

    def load_kv(bi, dil, r, m):
        s = cnt["kv"] % 8
        cnt["kv"] += 1
        base = 2048 + m * 128 * dil + r
        P.add("sp", f_dma(kvt[s], qkv_rows(base, dil, 512, ROW)), r=["QKVall"], w=["kvt%d" % s], dma="kvt%d" % s)
        tb = (0, 5)[cnt["tr"] % 2]
        cnt["tr"] += 1
        for p_ in range(4):
            P.add("pe", f_tr(psb[tb][:, p_ * 128:(p_ + 1) * 128], kvt[s][:, p_ * 128:(p_ + 1) * 128], identb),
                  r=["kvt%d" % s, "identb"], w=["ps%d" % tb])
        P.add("act", f_copy(KT[s], psb[tb][:, 0:512]), r=["ps%d" % tb], w=["KT%d" % s])
        return s

    BL = enable.get("B_lvl", 9)
    def stage1a(bi, dil, r, m):
        qs = cnt["q"] % 2
        cnt["q"] += 1
        base = 2048 + m * 128 * dil + r
        P.add("sp", f_dma(qt[qs], qkv_rows(base, dil, 0, 512)), r=["QKVall"], w=["qt%d" % qs], dma="qt%d" % qs)
        tb = (0, 5)[cnt["tr"] % 2]
        cnt["tr"] += 1
        for p_ in range(4):
            P.add("pe", f_tr(psb[tb][:, p_ * 128:(p_ + 1) * 128], qt[qs][:, p_ * 128:(p_ + 1) * 128], identb),
                  r=["qt%d" % qs, "identb"], w=["ps%d" % tb])
        P.add("dve", f_copy(QTe[qs][0:64, :], psb[tb][0:64, 0:512]), r=["ps%d" % tb], w=["QT%d" % qs])
        P.add("dve", f_copy(QTo[qs][64:128, :], psb[tb][64:128, 0:512]), r=["ps%d" % tb], w=["QT%d" % qs])
        return qs

    def stage1b(info):
        bi, dil, r, m, scur, sprev, qs = info
        bs_ = cnt["blk"] % 2
        cnt["blk"] += 1
        for half in range(2):
            banks = (1, 2) if half == 0 else (3, 4)
            for hh in range(2):
                bk = banks[hh]
                for h2 in range(2):
                    h = half * 4 + hh * 2 + h2
                    cs_ = slice((h // 2) * 128, (h // 2) * 128 + 128)
                    QTm_ = (QTe if h % 2 == 0 else QTo)[qs]
                    P.add("pe", f_mm(ps[bk][:, h2 * 256:h2 * 256 + 128], KT[scur][:, cs_], QTm_[:, cs_],
                                     True, True),
                          r=["KT%d" % scur, "QT%d" % qs], w=["ps%d" % bk])
                    P.add("pe", f_mm(ps[bk][:, h2 * 256 + 128:h2 * 256 + 256], KT[sprev][:, cs_],
                                     QTm_[:, cs_], True, True),
                          r=["KT%d" % sprev, "QT%d" % qs], w=["ps%d" % bk])
                hb = half * 4 + hh * 2
                P.add("act", f_act(Pexp[bs_][:, hb * 256:(hb + 2) * 256], ps[bk], AF.Exp, bias=negm[:, 0:1],
                                   scale=1.0),
                      r=["ps%d" % bk, "negm"], w=["Pexp%d_%d" % (bs_, half)])
            hsl = slice(half * 1024, (half + 1) * 1024)
            P.add("dve", f_tt(PTm[bs_][:, hsl], Pexp[bs_][:, hsl],
                              EB[:, bi, half * 4:(half + 1) * 4, :].rearrange("p h q -> p (h q)"), ALU.mult),
                  r=["Pexp%d_%d" % (bs_, half), "EB"], w=["PTm%d_%d" % (bs_, half)])
        return (bi, dil, r, m, scur, sprev, bs_)

    def stage2(st2):
        bi, dil, r, m, scur, sprev, bs_ = st2
        for half in range(2):
            ob = 6 + half
            for h4 in range(4):
                h = half * 4 + h4
                vcur = kvt[scur][:, 512 + h * 65:512 + (h + 1) * 65]
                vprev = kvt[sprev][:, 512 + h * 65:512 + (h + 1) * 65]
                pc = PTm[bs_][:, h * 256:h * 256 + 128]
                pp = PTm[bs_][:, h * 256 + 128:h * 256 + 256]
                o_ = ps[ob][0:65, h4 * 128:(h4 + 1) * 128]
                P.add("pe", f_mm(o_, vprev, pp, True, False), r=["kvt%d" % sprev, "PTm%d_%d" % (bs_, half)],
                      w=["ps%d" % ob])
                P.add("pe", f_mm(o_, vcur, pc, False, True), r=["kvt%d" % scur, "PTm%d_%d" % (bs_, half)],
                      w=["ps%d" % ob])
            t0 = m * 128 * dil + r
            osl = Oacc[0:65, half * 4:(half + 1) * 4, t0:t0 + 127 * dil + 1:dil]
            pin = ps[ob][0:65, :].rearrange("p (h t) -> p h t", h=4)
            if bi == 0:
                P.add("act", f_copy(osl, pin), r=["ps%d" % ob], w=["Oacc%d" % half])
            else:
                P.add("dve", f_tt(osl, osl, pin, ALU.add), r=["ps%d" % ob, "Oacc%d" % half],
                      w=["Oacc%d" % half])

    pendA = None
    pend2 = None
    sprev = None
    for bi, dil in enumerate(DILS[:enable.get("B_nbr", 3)]):
        nblk = 16 // dil
        for r in range(dil):
            for m in range(min(nblk, enable.get("B_nm", 99))):
                if m == 0:
                    sprev = load_kv(bi, dil, r, -1)
                scur = load_kv(bi, dil, r, m)
                qs = stage1a(bi, dil, r, m)
                infoA = (bi, dil, r, m, scur, sprev, qs)
                if pendA is not None:
                    cur2 = stage1b(pendA)
                    if pend2 is not None:
                        stage2(pend2)
                    pend2 = cur2
                pendA = infoA
                sprev = scur
    if pendA is not None:
        cur2 = stage1b(pendA)
        if pend2 is not None:
            stage2(pend2)
        pend2 = cur2
    if pend2 is not None:
        stage2(pend2)
    if enable.get("stop_at") == "B":
        P.emit()
        return nc, P
    att_st = [Pexp[1]] * 2
    rcp2 = [Pexp[0][:, 0:1024].bitcast(F32), Pexp[0][:, 1024:2048].bitcast(F32)]
    for h in range(8):
        as_ = 0
        for q4 in range(4):
            tsl = slice(q4 * 512, (q4 + 1) * 512)
            bk = 1 + (h * 4 + q4) % 4
            P.add("pe", f_mm(ps[bk][0:64, :], e64[:, 0:64], Oacc[:, h, tsl], True, True),
                  r=["Oacc0", "Oacc1", "e64"], w=["ps%d" % bk])
            rk = (h * 4 + q4) % 2
            rcp = rcp2[rk]
            P.add("dve", f_recip(rcp[0:64, :], ps[bk][0:64, :]), r=["ps%d" % bk, "Pexp0_0", "Pexp0_1", "PTm0_0", "PTm0_1"],
                  w=["rcp%d" % rk])
            P.add("pool", f_tt(att_st[as_][0:64, tsl], Oacc[0:64, h, tsl], rcp[0:64, :], ALU.mult),
                  r=["rcp%d" % rk, "Oacc0", "Oacc1", "Pexp1_0", "Pexp1_1", "PTm1_0", "PTm1_1"], w=["att_st%d_%d" % (as_, q4)])
        P.add("sp", f_dma(ATT[:, h * T:(h + 1) * T], att_st[as_][0:64, :]), r=["att_st%d_%d" % (as_, q) for q in range(4)],
              w=["ATT"], dma="attw%d" % as_)
    P.fence()

    if enable.get("stop_at") == "norm":
        P.emit()
        return nc, P
    AX_.reset()
    AY_.reset()
    x1T = AX_.get(17 * 1024, BF16).rearrange("p (t c k) -> p t c k", t=17, c=8)
    wup = [AX_.get(8 * 256, BF16).rearrange("p (c f) -> p c f", c=8) for _ in range(2)]
    wdn = [AX_.get(2 * 1024, BF16).rearrange("p (c n) -> p c n", c=2) for _ in range(2)]
    hT = AX_.get(2 * 2176, BF16).rearrange("p (c t) -> p c t", c=2)
    yacc = AY_.get(17 * 1024, F32).rearrange("p (t n) -> p t n", t=17)
    att_t2 = [sb("att_t%d" % i, 512, BF16).rearrange("p (c t) -> p c t", c=4) for i in range(2)]
    xf2 = [sb("xf%d" % i, 1024, F32) for i in range(2)]
    rr_ = sb("rr", 1024, F32)
    x1b = sb("x1b", 1024, BF16)
    gt2 = [sb("gt%d" % i, 512, BF16) for i in range(2)]
    ot2 = [sb("ot%d" % i, 512, F32) for i in range(2)]
    yrb2 = [sb("yrb%d" % i, 512, BF16) for i in range(2)]
    relu_t = [AX_.get(512, F32) for i in range(2)]
    for c in range(8):
        P.add("pool", f_dma(w_out_sb[:, c, :], w_out[c * 128:(c + 1) * 128, :]), w=["w_out"], dma="w_out")
    P.add("sp", f_dma(lng, bass.AP(ln1g, 0, [[0, 128], [1, 1024]])), w=["lng"], dma="lng")
    P.add("sp", f_dma(lnb, bass.AP(ln1b, 0, [[0, 128], [1, 1024]])), w=["lnb"], dma="lnb")

    def layer_norm(src, dst, eps):
        P.add("dve", f_bnstats(stats[:, 0:6], src[:, 0:512]), r=[src_name[0]], w=["stats"])
        P.add("dve", f_bnstats(stats[:, 6:12], src[:, 512:1024]), r=[src_name[0]], w=["stats"])
        P.add("dve", f_bnaggr(stats[:, 12:14], stats[:, 0:12]), r=["stats"], w=["stats"])
        P.add("dve", f_ts(stats[:, 14:15], stats[:, 13:14], eps, None, ALU.add), r=["stats"], w=["stats"])
        P.add("act", f_act(stats[:, 14:15], stats[:, 14:15], AF.Sqrt), r=["stats"], w=["stats"])
        P.add("dve", f_recip(stats[:, 14:15], stats[:, 14:15]), r=["stats"], w=["stats"])
        P.add("dve", f_stt(stats[:, 15:16], stats[:, 12:13], -1.0, stats[:, 14:15], ALU.mult, ALU.mult),
              r=["stats"], w=["stats"])
        P.add("dve", f_ts(dst, src, stats[:, 14:15], stats[:, 15:16], ALU.mult, ALU.add),
              r=["stats", src_name[0]], w=[src_name[1]])
        P.add(aff_eng[0], f_tt(dst, dst, lng, ALU.mult), r=[src_name[1], "lng"], w=[src_name[1]])
        P.add(aff_eng[0], f_tt(dst, dst, lnb, ALU.add), r=[src_name[1], "lnb"], w=[src_name[1]])

    src_name = ["rr", "rr"]
    aff_eng = ["dve"]
    attv = ATT.rearrange("d (h t) -> d h t", h=8)

    def prefetchD(ti):
        own = ti < 16
        sl = ti % 2
        if own:
            P.add("sp", f_dma(ot2[sl], OL[ti * 128:(ti + 1) * 128, :]), r=["OL%d" % ti], w=["ot%d" % sl], dma="olt%d" % sl)
            for par in range(2):
                P.add("sp", f_dma(att_t2[sl][par * 64:(par + 1) * 64, :, :], attv[:, par:8:2, ti * 128:(ti + 1) * 128]),
                      r=["ATT"], w=["att_t%d_%d" % (sl, par)], dma="att_t%d_%d" % (sl, par))
        grow = ti * 128 if own else T
        P.add("sp", f_dma(gt2[sl], GS[grow:grow + 128, :]), r=["GS%d" % (grow // 128)], w=["gt%d" % sl], dma="gt%d" % sl)
        xsrc = xo[ti * 128:(ti + 1) * 128, :] if own else xs
        P.add("sp", f_dma(xf2[sl], xsrc), w=["xf%d" % sl], dma="xf%d" % sl)

    def stageD1(ti):
        own = ti < 16
        sl = ti % 2
        gt = gt2[sl]
        yrb = yrb2[sl]
        if own:
            osrc = ot2[sl]
            oname = "ot%d" % sl
        else:
            osrc = oret_s
            oname = "oret_s"
        for h in range(4):
            hs = slice(h * 128, (h + 1) * 128)
            P.add("dve", f_bnstats(stats[:, 16 + h * 6:22 + h * 6], osrc[:, hs]), r=[oname], w=["gstats"])
            P.add("dve", f_bnaggr(stats[:, 40 + 2 * h:42 + 2 * h], stats[:, 16 + h * 6:22 + h * 6]), r=["gstats"],
                  w=["gstats"])
        P.add("dve", f_ts(stats[:, 48:52], stats[:, 41:49:2], 1e-6, None, ALU.add), r=["gstats"], w=["gstats"])
        P.add("act", f_act(stats[:, 48:52], stats[:, 48:52], AF.Sqrt), r=["gstats"], w=["gstats"])
        P.add("dve", f_recip(stats[:, 48:52], stats[:, 48:52]), r=["gstats"], w=["gstats"])
        P.add("dve", f_stt(stats[:, 52:56], stats[:, 40:48:2], -1.0, stats[:, 48:52], ALU.mult, ALU.mult),
              r=["gstats"], w=["gstats"])
        for h in range(4):
            hs = slice(h * 128, (h + 1) * 128)
            P.add("dve", f_ts(osrc[:, hs], osrc[:, hs], stats[:, 48 + h:49 + h], stats[:, 52 + h:53 + h],
                              ALU.mult, ALU.add), r=["gstats", oname], w=[oname])
        P.add("pool", f_tt(yrb, osrc, gt, ALU.mult), r=[oname, "gt%d" % sl], w=["yrb%d" % sl])
        for h in range(4):
            P.add("pe", f_tr(psb[5][:, h * 128:(h + 1) * 128], yrb[:, h * 128:(h + 1) * 128], identb),
                  r=["yrb%d" % sl, "identb"], w=["ps5"])
        P.add("act", f_copy(QdecT[:, :, ti * 128:(ti + 1) * 128],
                            psb[5][:, 0:512].rearrange("p (h t) -> p h t", h=4)), r=["ps5"], w=["QdecT%d" % ti])

    x1b2 = [x1b, S_f.bitcast(BF16)]

    def stageD2b(ti):
        sl_ = ti % 2
        xb_ = x1b2[sl_]
        for c in range(8):
            P.add("pe", f_tr(psb[0][:, c * 128:(c + 1) * 128], xb_[:, c * 128:(c + 1) * 128], identb),
                  r=["x1b%d" % sl_, "identb"], w=["ps0"])
        P.add("act", f_copy(x1T[:, ti, :, :].rearrange("p c k -> p (c k)"), psb[0]), r=["ps0"], w=["x1T"])

    prefetchD(0)
    prefetchD(1)
    stageD1(0)
    for ti in range(17):
        own = ti < 16
        sl = ti % 2
        if ti + 2 < 17:
            pass
        if ti + 1 < 17:
            stageD1(ti + 1)
        xf = xf2[sl]
        if own:
            a_t = att_t2[sl]
            anames = ["att_t%d_0" % sl, "att_t%d_1" % sl]
        else:
            a_t = att_s
            anames = ["att_s"]
        mb = (1, 2) if ti % 2 == 0 else (3, 4)
        for half in range(2):
            nsl = slice(half * 512, (half + 1) * 512)
            for c in range(4):
                P.add("pe", f_mm(ps[mb[half]], a_t[:, c, :], w_out_sb[:, c, nsl], c == 0, False),
                      r=anames + ["w_out"], w=["ps%d" % mb[half]])
            for c in range(4):
                P.add("pe", f_mm(ps[mb[half]], QdecT[:, c, ti * 128:(ti + 1) * 128], w_out_sb[:, 4 + c, nsl], False,
                                 c == 3),
                      r=["QdecT%d" % ti, "w_out"], w=["ps%d" % mb[half]])
            P.add("dve", f_stt(rr_[:, nsl], xf[:, nsl], float(ALPHA), ps[mb[half]], ALU.mult, ALU.add),
                  r=["xf%d" % sl, "ps%d" % mb[half]], w=["rr"])
        layer_norm(rr_, rr_, 1e-5)
        P.add("act", f_act(yacc[:, ti, :], rr_, AF.Copy, scale=float(ALPHA)), r=["rr"], w=["yacc%d" % ti])
        x1b_ = x1b2[sl]
        P.add("act", f_copy(x1b_, rr_), r=["rr"], w=["x1b%d" % sl])
        if ti >= 1:
            stageD2b(ti - 1)
        if ti + 2 < 17:
            prefetchD(ti + 2)
    stageD2b(16)

    if enable.get("stop_at") == "D":
        P.emit()
        return nc, P
    aff_eng[0] = "pool"
    P.add("sp", f_dma(lng, bass.AP(ln2g, 0, [[0, 128], [1, 1024]])), w=["lng"], dma="lng")
    P.add("sp", f_dma(lnb, bass.AP(ln2b, 0, [[0, 128], [1, 1024]])), w=["lnb"], dma="lnb")
    groups = [(0, 4), (4, 8), (8, 12), (12, 16), (16, 17)]
    w_up_v = w_up.rearrange("(c p) f -> p c f", p=128)
    hcnt = 0
    ycnt = 0
    for fb in range(16):
        s = fb % 2
        P.add("pool", f_dma(wup[s], w_up_v[:, :, fb * 256:(fb + 1) * 256]), w=["wup%d" % s], dma="wup%d" % s)
        P.add("pool", f_dma(wdn[s], w_down[fb * 256:(fb + 1) * 256, :].rearrange("(c p) n -> p c n", p=128)),
              w=["wdn%d" % s], dma="wdn%d" % s)
        def ffn_h(g0, g1):
            nonlocal hcnt
            ntok = (g1 - g0) * 128
            for fc in range(2):
                hb = (1, 2, 3)[hcnt % 3]
                rs = hcnt % 2
                hcnt += 1
                for c in range(8):
                    P.add("pe", f_mm(ps[hb][:, 0:ntok], wup[s][:, c, fc * 128:(fc + 1) * 128],
                                     x1T[:, g0:g1, c, :], c == 0, c == 7),
                          r=["wup%d" % s, "x1T"], w=["ps%d" % hb])
                P.add("act", f_act(relu_t[rs][:, 0:ntok], ps[hb][:, 0:ntok], AF.Relu), r=["ps%d" % hb],
                      w=["relu%d" % rs])
                P.add("pool", f_tt(hT[:, fc, g0 * 128:g1 * 128], relu_t[rs][:, 0:ntok], relu_t[rs][:, 0:ntok],
                                   ALU.mult), r=["relu%d" % rs], w=["hT%d_%d" % (fc, g0)])

        def ffn_y(g0, g1):
            nonlocal ycnt
            for ti in range(g0, g1):
                for half in range(2):
                    nsl = slice(half * 512, (half + 1) * 512)
                    yb = (4, 5, 6, 7)[ycnt % 4]
                    ycnt += 1
                    for fc in range(2):
                        P.add("pe", f_mm(ps[yb], hT[:, fc, ti * 128:(ti + 1) * 128], wdn[s][:, fc, nsl], fc == 0,
                                         fc == 1),
                              r=["hT%d_%d" % (fc, g0), "wdn%d" % s], w=["ps%d" % yb])
                    P.add("dve", f_tt(yacc[:, ti, nsl], yacc[:, ti, nsl], ps[yb], ALU.add),
                          r=["ps%d" % yb, "yacc%d" % ti], w=["yacc%d" % ti])
                if fb == 15:
                    src_name[0] = "yacc%d" % ti
                    src_name[1] = "yacc%d" % ti
                    layer_norm(yacc[:, ti, :], yacc[:, ti, :], 1e-5)
                    if ti < 16:
                        P.add("sp", f_dma(y_p[ti * 128:(ti + 1) * 128, :], yacc[:, ti, :]), r=["yacc%d" % ti],
                              dma="ypo", out=True)
                    else:
                        P.add("sp", f_dma(y_s, yacc[0:4, ti, :]), r=["yacc%d" % ti], dma="yso", out=True)

        for gi, (g0, g1) in enumerate(groups):
            ffn_h(g0, g1)
            if gi > 0:
                ffn_y(*groups[gi - 1])
        ffn_y(*groups[-1])
    P.emit()
    return nc, P


def _t5_bucket(dist):
    dist = np.asarray(dist)
    d_f = np.maximum(dist, 1).astype(np.float32)
    large = 16 + (np.log(d_f / np.float32(16)) / np.float32(math.log(2048 / 16)) * np.float32(16)).astype(np.int32)
    large = np.minimum(large, 31)
    return np.where(dist < 16, dist, large)


def _constants(core):
    c = {}
    c["c_ident"] = np.eye(128, dtype=np.float32)
    c["c_jflip"] = np.ascontiguousarray(np.eye(128, dtype=np.float32)[::-1])
    half = 64
    inv_freq = (np.float32(1.0) / (np.float32(10000.0) ** np.linspace(0.0, 1.0, half, dtype=np.float32))).astype(np.float32)
    cs = np.zeros((NHALO + 17, 128, 2, 64), np.float32)
    for ti in range(NHALO + 17):
        if ti < NHALO + 16:
            pos = (core * T - NHALO * 128 + ti * 128 + np.arange(128)).astype(np.float32)
        else:
            pos = np.full(128, 16384, np.float32)
        ang = (pos[:, None] * inv_freq[None, :]).astype(np.float32)
        cs[ti, :, 0, :] = np.cos(ang)
        cs[ti, :, 1, :] = np.sin(ang)
    c["c_cs"] = cs.reshape((NHALO + 17) * 128, 128)
    vt = np.ones((128, 33), np.float32)
    if core == 0:
        vt[:, 0:16] = 0.0
    c["c_vtab"] = vt
    lg = np.log(np.array(GAM, np.float64))
    kk = np.arange(128)[:, None]
    qq = np.arange(128)[None, :]
    dmt = np.zeros((128, 4, 128), np.float64)
    qdec = np.zeros((128, 4, 128), np.float64)
    kdec = np.zeros((128, 4, 128), np.float64)
    gam = np.zeros((128, 4, 128), np.float64)
    sc = 128.0 ** -0.5
    for h in range(4):
        dmt[:, h, :] = np.where(qq >= kk, np.exp(lg[h] * np.maximum(qq - kk, 0)), 0.0) * sc
        qdec[:, h, :] = np.exp(lg[h] * (np.arange(128) + 1.0))[None, :]
        kdec[:, h, :] = (np.exp(lg[h] * (127.0 - np.arange(128))) * sc)[:, None]
        gam[:, h, :] = GAM[h]
    c["c_dmt"] = dmt.reshape(128, 512).astype(np.float32)
    c["c_qdec"] = qdec.reshape(128, 512).astype(np.float32)
    c["c_kdec"] = kdec.reshape(128, 512).astype(np.float32)
    c["c_gam"] = gam.reshape(128, 512).astype(np.float32)
    oh = np.zeros((32, 1152 + 387), np.float32)
    fm = np.zeros((8, 1152), np.float32)
    for i, dil in enumerate(DILS):
        for idx in range(384):
            rel = idx - 127
            if 0 <= rel <= 128:
                oh[int(_t5_bucket(rel * dil)), i * 384 + idx] = 1.0
                fm[:, i * 384 + idx] = 1.0
        for jj in range(129):
            tap = 128 - jj if jj < 128 else 0
            oh[int(_t5_bucket(tap * dil)), 1152 + i * 129 + jj] = 1.0
    c["c_oh"] = oh
    c["c_fmask"] = fm
    d = np.zeros((128, 4), np.float32)
    d[0:4, 0:4] = np.eye(4)
    c["c_delta"] = d
    dr = np.zeros((128, 4, 4), np.float32)
    for b in range(4):
        dr[:, b, b] = 1.0
    c["c_drow"] = dr.reshape(128, 16)
    return c


ENABLE = {"samp": True}
_CACHE = {}


def kernel(x_prompt, x_sample, cache_kv_win, state_ret, w_in, rel_bias, w_out,
           ln1_g, ln1_b, w_up, w_down, ln2_g, ln2_b):
    f = lambda a: np.ascontiguousarray(np.asarray(a, dtype=np.float32))
    x_prompt, x_sample, cache_kv_win, state_ret = f(x_prompt), f(x_sample), f(cache_kv_win), f(state_ret)
    if "nc" not in _CACHE:
        _CACHE["nc"] = build(ENABLE)
    nc, P = _CACHE["nc"]
    xp = x_prompt[0]
    in_maps = []
    for c in range(NCORES):
        m = {}
        m["xo"] = xp[c * T:(c + 1) * T]
        xh_ = np.zeros((NHALO * 128, 1024), np.float32)
        lo = c * T - NHALO * 128
        if c > 0:
            xh_[max(0, -lo):] = xp[max(lo, 0):c * T]
        m["xh"] = xh_
        xs = np.zeros((128, 1024), np.float32)
        xs[0:4] = x_sample[c * 4:(c + 1) * 4, 0]
        m["xs"] = xs
        m["cache"] = cache_kv_win[0, c * 4:(c + 1) * 4].reshape(4, 2048, 1024)
        m["state"] = state_ret[0, c * 4:(c + 1) * 4]
        m["w_in"] = f(w_in)[0]
        m["relb"] = np.ascontiguousarray(np.tile(f(rel_bias), (1, 4)))
        m["w_out"] = f(w_out)[0]
        m["ln1g"] = f(ln1_g)
        m["ln1b"] = f(ln1_b)
        m["w_up"] = f(w_up)[0]
        m["w_down"] = f(w_down)[0]
        m["ln2g"] = f(ln2_g)
        m["ln2b"] = f(ln2_b)
        m.update(_constants(c))
        in_maps.append({k: np.ascontiguousarray(v) for k, v in m.items()})
    res = run_bass_kernel_spmd(nc, in_maps, core_ids=list(range(NCORES)))
    R = res.results
    y_prompt = np.concatenate([R[c]["y_p"] for c in range(NCORES)], axis=0)[None]
    y_sample = np.concatenate([R[c]["y_s"] for c in range(NCORES)], axis=0)[:, None, :]
    kv_win_prompt = R[7]["kvp"].reshape(1, 1, 2048, 2, 8, 64)
    kv_win_sample = np.concatenate([R[c]["kvs"] for c in range(NCORES)], axis=0).reshape(1, 32, 1, 2, 8, 64)
    state_ret_prompt = R[7]["srp"].reshape(1, 1, 4, 128, 128)
    state_ret_sample = np.concatenate([R[c]["srs"] for c in range(NCORES)], axis=0)[None]
    return (y_prompt.astype(np.float32), y_sample.astype(np.float32), kv_win_prompt.astype(np.float32),
            kv_win_sample.astype(np.float32), state_ret_prompt.astype(np.float32),
            state_ret_sample.astype(np.float32))
```
